# Optimizing a Trainium2 kernel written in Bass

```python
import jax, jax.numpy as jnp
from jax import lax
import numpy as np

D_MODEL = 2048
BATCH = 16
SEQ = 2048
DEPTH = 1
DEC_BATCH = 8
DEC_SEQ = 32
PAST_LEN = 1024

CHUNK = 64
MIX_WIDTH = D_MODEL
D_CONV = MIX_WIDTH // 2
N_CONV_GROUPS = 16
D_GMLP = MIX_WIDTH - D_CONV
N_GMLP_GROUPS = 8
GMLP_GROUP_DIM = D_GMLP // N_GMLP_GROUPS
GMLP_LEN = 128
CONV_WIDTH = 3
D_FF = 5632
D_IN_PROJ = 3 * D_CONV + 2 * D_GMLP
EPS = 1e-6

kernel_name = 'hymba_conv_gmlp_macaron_stream_step'


def rms_norm(x, g):
    xf = x.astype(jnp.float32)
    y = xf * lax.rsqrt(jnp.mean(xf * xf, axis=-1, keepdims=True) + EPS)
    return (y * g.astype(jnp.float32)).astype(x.dtype)


def swiglu_half_step(x, g, w_gate, w_up, w_down):
    h = rms_norm(x, g)
    a = jax.nn.silu(h @ w_gate) * (h @ w_up)
    return x + 0.5 * (a @ w_down)


def causal_conv3(zp, w):
    T = zp.shape[1] - (CONV_WIDTH - 1)
    return w[0] * zp[:, 0:T] + w[1] * zp[:, 1:T + 1] + w[2] * zp[:, 2:T + 2]


def chunk_causal_mask():
    blk = np.arange(GMLP_LEN) // CHUNK
    return jnp.asarray(blk[:, None] >= blk[None, :])


def gmlp_spatial_gate(u, v, w_s, b_s):
    bsz, T, _ = v.shape
    n_chunks = -(-T // GMLP_LEN)
    pad = n_chunks * GMLP_LEN - T
    vp = jnp.pad(v, ((0, 0), (0, pad), (0, 0)))
    vp = vp.reshape(bsz, n_chunks, GMLP_LEN, N_GMLP_GROUPS, GMLP_GROUP_DIM)
    w = jnp.where(chunk_causal_mask()[None], w_s, jnp.zeros_like(w_s)).astype(v.dtype)
    mixed = jnp.einsum('gij,bcjgd->bcigd', w, vp) + b_s.T.astype(v.dtype)[None, None, :, :, None]
    mixed = mixed.reshape(bsz, n_chunks * GMLP_LEN, D_GMLP)[:, :T]
    return u * mixed


def parallel_mixing(x, conv_hist, norm_g, w_in, conv_w, v_norm_g, w_s, b_s, conv_out_g, gmlp_out_g, w_o):
    h = rms_norm(x, norm_g)
    proj = h @ w_in
    gate_b, gate_c, xc, u, v = jnp.split(
        proj, [D_CONV, 2 * D_CONV, 3 * D_CONV, 3 * D_CONV + D_GMLP], axis=-1)
    z = gate_c * xc
    zp = jnp.concatenate([conv_hist.astype(z.dtype), z], axis=1)
    y_conv = gate_b * causal_conv3(zp, conv_w)
    new_conv = zp[:, -(CONV_WIDTH - 1):]
    u = jax.nn.gelu(u)
    v = rms_norm(jax.nn.gelu(v), v_norm_g)
    y_gmlp = gmlp_spatial_gate(u, v, w_s, b_s)
    y = jnp.concatenate([rms_norm(y_conv, conv_out_g), rms_norm(y_gmlp, gmlp_out_g)], axis=-1) @ w_o
    return x + y, new_conv, v


def setup_inputs(seed: int = 0) -> dict:
    key = jax.random.key(seed)
    ks = iter(jax.random.split(key, 32))

    def nrm(shape, scale):
        return jax.random.normal(next(ks), shape, jnp.float32) * scale

    def gain(shape):
        return 1.0 + nrm(shape, 0.01)

    return {
        'x_prompt': nrm((BATCH, SEQ, D_MODEL), 1.0),
        'x_sample': nrm((DEC_BATCH, DEC_SEQ, D_MODEL), 1.0),
        'cache_conv': nrm((DEPTH, DEC_BATCH, CONV_WIDTH - 1, D_CONV), 1.0),
        'ffn1_norm': gain((DEPTH, D_MODEL)),
        'ffn1_w_gate': nrm((DEPTH, D_MODEL, D_FF), D_MODEL ** -0.5),
        'ffn1_w_up': nrm((DEPTH, D_MODEL, D_FF), D_MODEL ** -0.5),
        'ffn1_w_down': nrm((DEPTH, D_FF, D_MODEL), D_FF ** -0.5),
        'mix_norm': gain((DEPTH, D_MODEL)),
        'w_in': nrm((DEPTH, D_MODEL, D_IN_PROJ), D_MODEL ** -0.5),
        'conv_w': nrm((DEPTH, CONV_WIDTH, D_CONV), CONV_WIDTH ** -0.5),
        'gmlp_v_norm': gain((DEPTH, D_GMLP)),
        'gmlp_w_s': nrm((DEPTH, N_GMLP_GROUPS, GMLP_LEN, GMLP_LEN), GMLP_LEN ** -0.5),
        'gmlp_b': 1.0 + nrm((DEPTH, N_GMLP_GROUPS, GMLP_LEN), 0.1),
        'conv_out_norm': gain((DEPTH, D_CONV)),
        'gmlp_out_norm': gain((DEPTH, D_GMLP)),
        'w_o': nrm((DEPTH, MIX_WIDTH, D_MODEL), MIX_WIDTH ** -0.5),
        'ffn2_norm': gain((DEPTH, D_MODEL)),
        'ffn2_w_gate': nrm((DEPTH, D_MODEL, D_FF), D_MODEL ** -0.5),
        'ffn2_w_up': nrm((DEPTH, D_MODEL, D_FF), D_MODEL ** -0.5),
        'ffn2_w_down': nrm((DEPTH, D_FF, D_MODEL), D_FF ** -0.5),
        'final_norm': gain((D_MODEL,)),
    }


def reference(x_prompt, x_sample, cache_conv, ffn1_norm, ffn1_w_gate, ffn1_w_up, ffn1_w_down,
              mix_norm, w_in, conv_w, gmlp_v_norm, gmlp_w_s, gmlp_b, conv_out_norm, gmlp_out_norm,
              w_o, ffn2_norm, ffn2_w_gate, ffn2_w_up, ffn2_w_down, final_norm):
    def trunk(x, conv_hist_all):
        conv_states, v_rows = [], []
        for l in range(DEPTH):
            x = swiglu_half_step(x, ffn1_norm[l], ffn1_w_gate[l], ffn1_w_up[l], ffn1_w_down[l])
            x, conv_state, v = parallel_mixing(
                x, conv_hist_all[l], mix_norm[l], w_in[l], conv_w[l], gmlp_v_norm[l],
                gmlp_w_s[l], gmlp_b[l], conv_out_norm[l], gmlp_out_norm[l], w_o[l])
            x = swiglu_half_step(x, ffn2_norm[l], ffn2_w_gate[l], ffn2_w_up[l], ffn2_w_down[l])
            conv_states.append(conv_state)
            v_rows.append(v)
        return rms_norm(x, final_norm), jnp.stack(conv_states), jnp.stack(v_rows)

    zero_hist = jnp.zeros((DEPTH, x_prompt.shape[0], CONV_WIDTH - 1, D_CONV), x_prompt.dtype)
    y_prompt, conv_state_prompt, _ = trunk(x_prompt, zero_hist)
    y_sample, conv_state_sample, gmlp_v_sample = trunk(x_sample, cache_conv)
    return (y_prompt, y_sample, conv_state_prompt, conv_state_sample, gmlp_v_sample)
```

```python
import numpy as np
import concourse.bass as bass
import concourse.mybir as mybir
from concourse.bass_utils import run_bass_kernel_spmd

F32 = mybir.dt.float32
F32R = mybir.dt.float32r
BF16 = mybir.dt.bfloat16
AF = mybir.ActivationFunctionType
ALU = mybir.AluOpType
AX = mybir.AxisListType

EPS = 1e-6
FAST = True
BF16_MM = True


class Cfg:
    def __init__(self, D=2048, DFF=5632, NSEQ=2, SEQ=2048, DEC=32, fast=True, PT=6, MT=3, R=4):
        self.D, self.DFF, self.NSEQ, self.SEQ, self.DEC = D, DFF, NSEQ, SEQ, DEC
        self.fast = fast
        self.KC = D // 128
        self.DC = D // 2
        self.DG = D // 2
        self.NCC = self.DC // 128
        self.NG = self.DG // 128
        self.DIN = 3 * self.DC + 2 * self.DG
        self.NT = SEQ // 128
        self.PT = PT
        self.MT = MT
        self.R = R
        self.UW = 256
        self.NCOL = min(512, D)
        assert DFF % 256 == 0 and self.DC % 256 == 0 and D % 256 == 0


class SubTile:
    def __init__(self, seq, t, ntok, sample, nreal=None):
        self.seq, self.t, self.ntok, self.sample = seq, t, ntok, sample
        self.nreal = ntok if nreal is None else nreal


def make_passes(cfg):
    passes = []
    for s in range(cfg.NSEQ):
        nt = cfg.NT
        npass = -(-nt // cfg.PT)
        if s == cfg.NSEQ - 1 and nt % cfg.PT == 0:
            pass
        base = nt // npass
        extra = nt % npass
        sizes = [base + (1 if i < extra else 0) for i in range(npass)]
        t0 = 0
        for sz in sizes:
            passes.append([SubTile(s, t0 + i, 128, False) for i in range(sz)])
            t0 += sz
    smp = SubTile(cfg.NSEQ, 0, 128, True, nreal=cfg.DEC)
    if len(passes[-1]) < cfg.PT:
        passes[-1].append(smp)
    else:
        passes.append([smp])
    return passes


def split_sub(cfg, tiles):
    n = len(tiles)
    ng = -(-n // cfg.MT)
    out, i = [], 0
    base, extra = n // ng, n % ng
    for g in range(ng):
        sz = base + (1 if g < extra else 0)
        out.append(list(range(i, i + sz)))
        i += sz
    return out


def col_chunks(total, maxw=512):
    n = -(-total // maxw)
    w = -(-total // n)
    if w % 2:
        w += 1
    out, c = [], 0
    while c < total:
        out.append((c, min(total, c + w)))
        c += w
    return out


class SemC:
    def __init__(self, h, name):
        self.h, self.n, self.name = h, 0, name


ENGS = ["sync", "act", "dve", "pool", "pe"]


class Plan:
    def __init__(self):
        self.q = {e: [] for e in ENGS}
        self.seen = {e: {} for e in ENGS}
        self.own = {}
        self.block = []

    def op(self, eng, fn, waits=(), sem=None, amt=1, hard=()):
        q = self.q[eng]
        if self.block:
            waits = list(waits) + self.block
        waits = [(t, False) for t in waits] + [(t, True) for t in hard]
        for t, is_hard in waits:
            if t is None:
                continue
            s, v = t
            if s is self.own.get(eng) and not is_hard:
                continue
            if self.seen[eng].get(id(s), 0) >= v:
                continue
            self.seen[eng][id(s)] = v
            q.append((0, s, v))
        if sem is not None:
            sem.n += amt
            q.append((1, fn, sem, amt))
            return (sem, sem.n)
        q.append((1, fn, None, 0))
        return None


def emit(h, items):
    for it in items:
        if it[0] == 0:
            h.wait_ge(it[1].h, it[2])
        else:
            ins = it[1](h)
            if it[2] is not None:
                ins.then_inc(it[2].h, it[3])


class Unit:
    pass


class Ring:
    def __init__(self, K, seq):
        self.K, self.seq = K, seq
        self.next_load, self.next_use, self.released = 0, 0, 0
        self.rel = {}

    def plan_loads(self):
        K = self.K
        R = K.cfg.R
        while self.next_load < len(self.seq) and self.next_load < self.released + R:
            u = self.next_load
            r = u % R
            key, src, kd, n = self.seq[u]
            dst = K.RING[:, r, 0:kd * n].rearrange("p (k n) -> p k n", k=kd)
            waits = [self.rel[u - R]] if u >= R else []
            K.P.op(K.ring_eng, (lambda h, dst=dst, src=src: h.dma_start(out=dst, in_=src)),
                   waits=waits, sem=K.slot_ld[r], amt=16)
            self.next_load += 1

    def acquire(self, key):
        K = self.K
        u = self.next_use
        self.next_use += 1
        k2, src, kd, n = self.seq[u]
        assert k2 == key, (k2, key)
        assert u < self.next_load, "unit load not planned yet (ring too small for this schedule)"
        r = u % K.cfg.R
        un = Unit()
        un.idx = u
        un.v = K.RING[:, r, 0:kd * n].rearrange("p (k n) -> p k n", k=kd)
        un.ld = (K.slot_ld[r], 16 * (u // K.cfg.R + 1))
        return un

    def release(self, un, ticket):
        assert un.idx == self.released, (un.idx, self.released)
        self.rel[un.idx] = ticket
        self.released += 1
        self.plan_loads()


class Banks:
    def __init__(self, K, nb):
        self.K, self.nb, self.i = K, nb, 0
        self.reader = [None] * nb

    def get(self):
        b = self.i % self.nb
        self.i += 1
        return b, self.reader[b]

    def set_reader(self, b, ticket):
        self.reader[b] = ticket


class Ctx:
    pass


def build_program(cfg):
    nc = bass.Bass("TRN2", target_bir_lowering=False)
    nc.dge_precook = False
    bf = getattr(cfg, "bf16", BF16_MM)
    MM = BF16 if bf else (F32R if cfg.fast else F32)
    WDT = F32 if bf else MM
    D, DFF, KC, DC, DG, NCC, NG = cfg.D, cfg.DFF, cfg.KC, cfg.DC, cfg.DG, cfg.NCC, cfg.NG
    K = Ctx()
    K.cfg, K.nc = cfg, nc
    K.ring_eng = "pool" if getattr(cfg, "bf16", BF16_MM) else "sync"
    P = K.P = Plan()

    def din(name, shape, dt=F32):
        return nc.dram_tensor(name, list(shape), dt, kind="ExternalInput").ap()

    def dout(name, shape):
        return nc.dram_tensor(name, list(shape), F32, kind="ExternalOutput").ap()

    xp = din("xp", [cfg.NSEQ, cfg.SEQ, D])
    xs = din("xs", [cfg.DEC, D])
    cc = din("cc", [2, DC])
    wts = {}
    for nm, shp in [("wg1", [D, DFF]), ("wu1", [D, DFF]), ("wd1", [DFF, D]), ("win", [D, cfg.DIN]),
                    ("wo", [D, D]), ("wg2", [D, DFF]), ("wu2", [D, DFF]), ("wd2", [DFF, D])]:
        wts[nm] = din(nm, shp, WDT)
    g1d, g2d, g3d, gfd = din("g1", [D]), din("g2", [D]), din("g3", [D]), din("gf", [D], WDT)
    cwd = din("cw", [3, DC])
    gvd = din("gv", [DG])
    wsd = din("ws", [NG, 128, 128])
    bsd = din("bs", [NG, 128])
    gad, gbd = din("ga", [DC]), din("gb", [DG])
    yp = dout("yp", [cfg.NSEQ, cfg.SEQ, D])
    ys = dout("ys", [cfg.DEC, D])
    csp = dout("csp", [cfg.NSEQ, 2, DC])
    css = dout("css", [2, DC])
    gvs = dout("gvs", [cfg.DEC, DG])

    passes = make_passes(cfg)
    PT, MT = cfg.PT, cfg.MT
    TPMAX = PT * 128
    TM = MT * 128

    A = nc.alloc_sbuf_tensor
    X = A("X", [128, PT, D], F32)
    regsz = max(KC * TPMAX + 8 * TPMAX, KC * TM + MT * DG + NCC * TM + NG * TM)
    REG = A("REG", [128, regsz], MM)
    XN = A("XN", [128, D], F32)
    TW = TM + 8
    TMPR = A("TMPR", [128, 3, TW], F32)
    SQT = A("SQT", [128, TW], MM)
    RING = K.RING = A("RING", [128, cfg.R, 16 * 256], MM)
    IDENT = A("IDENT", [128, 128], F32)
    ONES2 = A("ONES2", [128, 2], MM)
    G1, G2, G3 = A("G1", [128, KC], F32), A("G2", [128, KC], F32), A("G3", [128, KC], F32)
    GVBC = A("GVBC", [128, DG], F32)
    BB = A("BB", [128, NG, 128], F32)
    CW = A("CW", [128, NCC, 3], F32)
    GA, GB = A("GA", [128, NCC], F32), A("GB", [128, NG], F32)
    WMT = A("WMT", [128, NG, 128], MM)
    CARRY = A("CARRY", [128, cfg.NSEQ + 1, NCC, 2], F32)
    SS = A("SS", [128, PT], F32)
    SD = A("SD", [128, PT], F32)
    RS = A("RS", [128, PT], F32)
    ST2 = A("ST2", [128, 2, MT], F32)
    RAB = A("RAB", [128, 2, MT], F32)
    PS = nc.alloc_psum_tensor("PS", [128, 8, 512], F32)

    hT = REG[:, 0:KC * TPMAX].rearrange("p (k t) -> p k t", k=KC)
    aT = [REG[:, KC * TPMAX + i * 4 * TPMAX: KC * TPMAX + (i + 1) * 4 * TPMAX].rearrange("p (c t) -> p c t", c=4)
          for i in range(2)]
    o = 0
    hTm = REG[:, o:o + KC * TM].rearrange("p (k t) -> p k t", k=KC); o += KC * TM
    Vn = REG[:, o:o + MT * DG].rearrange("p (s d) -> p s d", s=MT); o += MT * DG
    yaT = REG[:, o:o + NCC * TM].rearrange("p (c t) -> p c t", c=NCC); o += NCC * TM
    ybT = REG[:, o:o + NG * TM].rearrange("p (c t) -> p c t", c=NG); o += NG * TM
    VnF = Vn if bf else Vn.bitcast(F32)

    sems = {}

    def mk(name):
        cm = nc.semaphore(name)
        h = cm.__enter__()
        sems[name] = cm
        return SemC(h, name)

    s_pe, s_act, s_dve, s_pool = mk("s_pe"), mk("s_act"), mk("s_dve"), mk("s_pool")
    P.own = {"pe": s_pe, "act": s_act, "dve": s_dve, "pool": s_pool}
    K.slot_ld = [mk(f"sl{r}") for r in range(cfg.R)]
    xld = [mk(f"xld{i}") for i in range(PT)]
    s_cst = mk("s_cst")
    s_yst = mk("s_yst")
    s_mst = mk("s_mst")
    s_dbg = mk("s_dbg")
    s_gb = mk("s_gb")

    UW = cfg.UW
    NGRP = DFF // UW

    def wview(w, c0, n):
        return w.rearrange("(k p) n -> p k n", p=128)[:, :, c0:c0 + n]

    def ffn_units(tag, wg, wu, wd):
        seq = []

        def gu(j):
            seq.append(((tag, "g", j), wview(wg, j * UW, UW), KC, UW))
            seq.append(((tag, "u", j), wview(wu, j * UW, UW), KC, UW))

        def dn(j):
            seq.append(((tag, "d", j), wd.rearrange("(c p) n -> p c n", p=128)[:, 2 * j:2 * j + 2, :], 2, D))

        assert NGRP % 2 == 0
        gu(0)
        gu(1)
        for J in range(NGRP // 2):
            if 2 * J + 2 < NGRP:
                gu(2 * J + 2)
                gu(2 * J + 3)
            dn(2 * J)
            dn(2 * J + 1)
        return seq

    def mixer_units(tag):
        seq = []
        win, wo = wts["win"], wts["wo"]
        for i in range(DG // UW):
            seq.append(((tag, "v", i), wview(win, 3 * DC + DG + i * UW, UW), KC, UW))
        for i in range(DC // UW):
            seq.append(((tag, "c", i), wview(win, DC + i * UW, UW), KC, UW))
            seq.append(((tag, "x", i), wview(win, 2 * DC + i * UW, UW), KC, UW))
            seq.append(((tag, "b", i), wview(win, i * UW, UW), KC, UW))
        for i in range(DG // UW):
            seq.append(((tag, "uu", i), wview(win, 3 * DC + i * UW, UW), KC, UW))
        for i in range(D // UW):
            seq.append(((tag, "o", i), wview(wo, i * UW, UW), KC, UW))
        return seq

    useq = []
    for pi, tiles in enumerate(passes):
        useq += ffn_units((pi, 1), wts["wg1"], wts["wu1"], wts["wd1"])
        for si, _ in enumerate(split_sub(cfg, tiles)):
            useq += mixer_units((pi, "m", si))
        useq += ffn_units((pi, 2), wts["wg2"], wts["wu2"], wts["wd2"])
    ring = K.ring = Ring(K, useq)
    ring.plan_loads()
    banks = Banks(K, 7)
    STATB = 7

    def dma_pool(out, in_, sem):
        return P.op("pool", (lambda h: h.dma_start(out=out, in_=in_)), sem=sem, amt=16)

    K.dumps = {}

    def dump(name, ap, waits):
        if not getattr(cfg, "dbg", False):
            return
        shp = list(ap.shape)
        d = nc.dram_tensor("dbg_" + name, shp, F32, kind="ExternalOutput").ap()
        t = P.op("pool", (lambda h: h.dma_start(out=d, in_=ap)), waits=waits, sem=s_dbg, amt=16)
        P.block.append(t)

    P.op("pool", lambda h: h.memset(IDENT[:], 0.0))
    t_ident = P.op("pool", lambda h: h.affine_select(out=IDENT[:], in_=IDENT[:], compare_op=ALU.not_equal, fill=1.0,
                                                     base=0, pattern=[[-1, 128]], channel_multiplier=1), sem=s_pool)
    ONESF = A("ONESF", [128, 64], F32)
    ZT = A("ZT", [128, 64], F32)
    P.op("pool", lambda h: h.memset(ONESF[:], 1.0))
    P.op("pool", lambda h: h.memset(ZT[:], 0.0))
    P.op("pool", lambda h: h.tensor_copy(out=ONES2[:], in_=ONESF[:, 0:2]))
    t_cz = P.op("pool", lambda h: h.memset(CARRY[:].rearrange("p s c r -> p (s c r)"), 0.0), sem=s_pool)
    for Gt, gd in ((G1, g1d), (G2, g2d), (G3, g3d)):
        for k in range(KC):
            dma_pool(Gt[:, k:k + 1], gd[k * 128:(k + 1) * 128].rearrange("(p o) -> p o", o=1), s_cst)
    for c in range(NCC):
        dma_pool(GA[:, c:c + 1], gad[c * 128:(c + 1) * 128].rearrange("(p o) -> p o", o=1), s_cst)
        for r in range(3):
            dma_pool(CW[:, c, r:r + 1], cwd[r, c * 128:(c + 1) * 128].rearrange("(p o) -> p o", o=1), s_cst)
    for c in range(NG):
        dma_pool(GB[:, c:c + 1], gbd[c * 128:(c + 1) * 128].rearrange("(p o) -> p o", o=1), s_cst)
    dma_pool(GVBC[:], gvd.partition_broadcast(128), s_cst)
    dma_pool(BB[:], bsd.partition_broadcast(128), s_cst)
    WSN = XN[:, 0:NG * 128].rearrange("p (g j) -> p g j", g=NG)
    dma_pool(WSN, wsd.rearrange("g i j -> i g j"), s_cst)
    for c in range(NCC):
        P.op("pool", (lambda h, c=c: h.dma_start(out=CARRY[:, cfg.NSEQ, c, :],
                                                  in_=cc[:, c * 128:(c + 1) * 128].rearrange("r p -> p r"))),
             waits=[t_cz], sem=s_cst, amt=16)
    t_cst = (s_cst, s_cst.n)
    t_w = None
    for g in range(NG):
        b, rd = banks.get()
        tj = P.op("pe", (lambda h, b=b, g=g: h.transpose(out=PS[:, b, 0:128], in_=WSN[:, g, :], identity=IDENT[:])),
                  waits=[t_cst, t_ident, rd], sem=s_pe)
        t_w = P.op("dve", (lambda h, b=b, g=g: h.tensor_copy(out=WMT[:, g, :], in_=PS[:, b, 0:128])),
                   waits=[tj], sem=s_dve)
        banks.set_reader(b, t_w)
    t_w = P.op("dve", lambda h: h.tensor_copy(out=WMT[64:128, :, 0:64],
                                               in_=ZT[64:128, :].unsqueeze(1).to_broadcast([64, NG, 64])),
               waits=[t_cz], sem=s_dve)
    xn_free = [t_w]

    x_free = [None] * PT
    carry_t = [t_cst] * (cfg.NSEQ + 1)

    def pe_job(mms, waits):
        t = None
        for i, fn in enumerate(mms):
            last = i == len(mms) - 1
            t = P.op("pe", fn, waits=waits if i == 0 else (), sem=s_pe if last else None)
        return t

    def norm_T(slots, Gt, dst, x_ready):
        last = None
        for (i, ntok, co) in slots:
            tq = P.op("act", (lambda h, i=i, ntok=ntok: h.activation(out=XN[:ntok, :], in_=X[:ntok, i, :],
                                                                      func=AF.Square, accum_out=SS[:ntok, i:i + 1])),
                      waits=list(x_ready[i]) + xn_free + [t_eps], sem=s_act)
            t1 = P.op("act", (lambda h, i=i, ntok=ntok: h.activation(out=SD[:ntok, i:i + 1], in_=SS[:ntok, i:i + 1],
                                                                      func=AF.Sqrt, scale=1.0 / D, bias=EPSB[:ntok, :])),
                      waits=[sd_free[0]], hard=[tq], sem=s_act)
            sd_free[0] = None
            t2 = P.op("dve", (lambda h, i=i, ntok=ntok: h.reciprocal(out=RS[:ntok, i:i + 1], in_=SD[:ntok, i:i + 1])),
                      waits=[t1], sem=s_dve)
            t3 = P.op("act", (lambda h, i=i, ntok=ntok: h.activation(out=XN[:ntok, :], in_=X[:ntok, i, :],
                                                                      func=AF.Identity, scale=RS[:ntok, i:i + 1])),
                      waits=[t2], sem=s_act)
            x_last_read[i] = t3
            tj = None
            for b0 in range(0, KC, 4):
                nq = min(4, KC - b0)
                b, rd = banks.get()
                mms = [(lambda h, b=b, q=q, k=b0 + q, ntok=ntok: h.transpose(
                    out=PS[:, b, q * 128:q * 128 + ntok], in_=XN[:ntok, k * 128:(k + 1) * 128],
                    identity=IDENT[:ntok, :ntok])) for q in range(nq)]
                tj = pe_job(mms, [t3, rd, t_ident])
                bv = PS[:, b, 0:nq * 128].rearrange("p (q t) -> p q t", q=nq)[:, :, 0:ntok]
                gv_ = Gt[:, b0:b0 + nq].unsqueeze(2).to_broadcast([128, nq, ntok])
                last = P.op("dve", (lambda h, bv=bv, gv_=gv_, b0=b0, nq=nq, co=co, ntok=ntok: h.tensor_tensor(
                    out=dst[:, b0:b0 + nq, co:co + ntok], in0=bv, in1=gv_, op=ALU.mult)),
                            waits=[tj, t_cst] + dst_free, sem=s_dve)
                banks.set_reader(b, last)
            xn_free[:] = [tj]
        return last

    EPSB = A("EPSB", [128, 1], F32)
    P.op("pool", lambda h: h.memset(EPSB[:], EPS))
    t_eps = P.op("pool", lambda h: h.memset(SS[:], 0.0), sem=s_pool)
    x_last_read = [None] * PT
    dst_free = []

    def ffn(tag, tiles, cols, Tp, ht_ticket, x_norm_read):
        halves = col_chunks(Tp)
        at_ready = {}
        at_ready_g = {}
        at_free = [None, None]
        tmp_free = [None, None]
        ti = [0]
        last_acc = [None]
        last_gu = [None]

        def GU(j):
            ug = ring.acquire((tag, "g", j))
            uu = ring.acquire((tag, "u", j))
            tG = tU = tM = None
            for c in range(2):
                for (h0, h1) in halves:
                    n = h1 - h0
                    bg, rdg = banks.get()
                    tG = pe_job([(lambda h, k=k, c=c, bg=bg, h0=h0, h1=h1, n=n: h.matmul(
                        PS[:, bg, 0:n], ug.v[:, k, c * 128:(c + 1) * 128], hT[:, k, h0:h1],
                        start=(k == 0), stop=(k == KC - 1))) for k in range(KC)], [ug.ld, ht_ticket, rdg])
                    bu, rdu = banks.get()
                    tU = pe_job([(lambda h, k=k, c=c, bu=bu, h0=h0, h1=h1, n=n: h.matmul(
                        PS[:, bu, 0:n], uu.v[:, k, c * 128:(c + 1) * 128], hT[:, k, h0:h1],
                        start=(k == 0), stop=(k == KC - 1))) for k in range(KC)], [uu.ld, ht_ticket, rdu])
                    tb = ti[0] % 2
                    ti[0] += 1
                    tS = P.op("act", (lambda h, tb=tb, bg=bg, n=n: h.activation(
                        out=TMPR[:, tb, 0:n], in_=PS[:, bg, 0:n], func=AF.Silu)),
                              waits=[tG, tmp_free[tb]], sem=s_act)
                    banks.set_reader(bg, tS)
                    tM = P.op("dve", (lambda h, tb=tb, bu=bu, n=n, c=c, h0=h0, h1=h1, j=j: h.tensor_tensor(
                        out=aT[(j // 2) % 2][:, (j % 2) * 2 + c, h0:h1], in0=TMPR[:, tb, 0:n], in1=PS[:, bu, 0:n],
                        op=ALU.mult)),
                              waits=[tS, tU, at_free[(j // 2) % 2]], sem=s_dve)
                    banks.set_reader(bu, tM)
                    tmp_free[tb] = tM
            ring.release(ug, tG)
            ring.release(uu, tU)
            at_ready_g[j] = tM
            last_gu[0] = tU

        def DN2(J):
            uds = [ring.acquire((tag, "d", 2 * J)), ring.acquire((tag, "d", 2 * J + 1))]
            tD = None
            NC_ = cfg.NCOL
            for (i, ntok, co) in cols:
                for n0 in range(0, D, NC_):
                    b, rd = banks.get()
                    tD = pe_job([(lambda h, c4=c4, b=b, ntok=ntok, co=co, n0=n0: h.matmul(
                        PS[:ntok, b, 0:NC_], aT[J % 2][:, c4, co:co + ntok], uds[c4 // 2].v[:, c4 % 2, n0:n0 + NC_],
                        start=(c4 == 0), stop=(c4 == 3))) for c4 in range(4)],
                                [uds[0].ld, uds[1].ld, at_ready[J], rd])
                    tA = P.op("dve", (lambda h, b=b, i=i, ntok=ntok, n0=n0: h.scalar_tensor_tensor(
                        out=X[:ntok, i, n0:n0 + NC_], in0=PS[:ntok, b, 0:NC_], scalar=0.5,
                        in1=X[:ntok, i, n0:n0 + NC_], op0=ALU.mult, op1=ALU.add)),
                              waits=[tD, x_norm_read], sem=s_dve)
                    banks.set_reader(b, tA)
                    last_acc[0] = tA
            ring.release(uds[0], tD)
            ring.release(uds[1], tD)
            at_free[J % 2] = tD

        GU(0)
        GU(1)
        at_ready2 = {}
        for J in range(NGRP // 2):
            at_ready[J] = at_ready_g[2 * J + 1]
            if 2 * J + 2 < NGRP:
                GU(2 * J + 2)
                GU(2 * J + 3)
            DN2(J)
        return last_acc[0], at_free, last_gu[0]

    out_tickets = []
    reg_free = []
    for pi, tiles in enumerate(passes):
        nsl = len(tiles)
        cols = [(i, st.ntok, 128 * i) for i, st in enumerate(tiles)]
        Tp = sum(st.ntok for st in tiles)
        x_ready = {}
        for i, st in enumerate(tiles):
            src = xs[:, :] if st.sample else xp[st.seq, st.t * 128:(st.t + 1) * 128, :]
            t = P.op("pool", (lambda h, i=i, st=st, src=src: h.dma_start(out=X[:st.nreal, i, :], in_=src)),
                     waits=[x_free[i]], sem=xld[i], amt=16)
            x_ready[i] = [t]
            if st.nreal < st.ntok:
                assert st.nreal == 32
                P.op("dve", (lambda h, i=i: h.memset(X[32:64, i, :], 0.0)), waits=[x_free[i]])
                tz = P.op("dve", (lambda h, i=i: h.memset(X[64:128, i, :], 0.0)), sem=s_dve)
                x_ready[i].append(tz)
        dst_free[:] = list(reg_free)
        t_ht = norm_T(cols, G1, hT, x_ready)
        if pi == 0:
            getattr(cfg, "dbg", False) and dump("hT1", hT.bitcast(F32)[:, :, 0:Tp], [t_ht])
        x_nr = x_last_read[cols[-1][0]]
        t_x, at_free, _ = ffn((pi, 1), tiles, cols, Tp, t_ht, x_nr)
        if pi == 0:
            getattr(cfg, "dbg", False) and dump("x1", X[:, 0:nsl, :], [t_x])
        reg_free = [t for t in at_free if t is not None]
        for si, idxs in enumerate(split_sub(cfg, tiles)):
            mtag = (pi, "m", si)
            sub = [tiles[i] for i in idxs]
            mcols = [(i, tiles[i].ntok, 128 * li) for li, i in enumerate(idxs)]
            Tm = sum(tiles[i].ntok for i in idxs)
            segs = []
            for li, i in enumerate(idxs):
                st = tiles[i]
                if segs and segs[-1]["seq"] == st.seq:
                    segs[-1]["n"] += st.ntok
                    segs[-1]["nr"] += st.nreal
                    segs[-1]["last"] = st
                else:
                    segs.append({"seq": st.seq, "c0": 128 * li, "n": st.ntok, "nr": st.nreal, "zo": 128 * li + 2 * len(segs) + 2,
                                 "first": st, "last": st})
            dst_free[:] = list(reg_free)
            xr = {i: [t_x] for i in idxs}
            t_hm = norm_T(mcols, G2, hTm, xr)
            if pi == 0 and si == 0:
                getattr(cfg, "dbg", False) and dump("hTm", hTm.bitcast(F32)[:, :, 0:Tm], [t_hm])
            x_nr = x_last_read[idxs[-1]]
            t_vn = {}
            for vi in range(DG // UW):
                un = ring.acquire((mtag, "v", vi))
                tj = None
                for li, (i, ntok, lc) in enumerate(mcols):
                    b, rd = banks.get()
                    tj = pe_job([(lambda h, k=k, b=b, ntok=ntok, lc=lc, un=un: h.matmul(
                        PS[:ntok, b, 0:UW], hTm[:, k, lc:lc + ntok], un.v[:, k, :],
                        start=(k == 0), stop=(k == KC - 1))) for k in range(KC)], [un.ld, t_hm, rd])
                    st = tiles[i]
                    if st.sample:
                        tg = P.op("act", (lambda h, b=b, ntok=ntok, vi=vi: h.activation(
                            out=XN[:ntok, vi * UW:(vi + 1) * UW], in_=PS[:ntok, b, 0:UW], func=AF.Gelu_apprx_tanh)),
                                  waits=[tj] + xn_free, sem=s_act)
                    else:
                        tg = P.op("act", (lambda h, b=b, ntok=ntok, li=li, vi=vi: h.activation(
                            out=Vn[:ntok, li, vi * UW:(vi + 1) * UW], in_=PS[:ntok, b, 0:UW], func=AF.Gelu_apprx_tanh)),
                                  waits=[tj] + dst_free, sem=s_act)
                    banks.set_reader(b, tg)
                ring.release(un, tj)
            for li, (i, ntok, lc) in enumerate(mcols):
                st = tiles[i]
                src = XN[:ntok, 0:DG] if st.sample else VnF[:ntok, li, :]
                junk = TMPR[:ntok].rearrange("p a b -> p (a b)")[:, 0:DG]
                tq = P.op("act", (lambda h, ntok=ntok, li=li, src=src, junk=junk: h.activation(
                    out=junk, in_=src, func=AF.Square, accum_out=SS[:ntok, li:li + 1])),
                          waits=xn_free, sem=s_act)
                t1 = P.op("act", (lambda h, ntok=ntok, li=li: h.activation(
                    out=SD[:ntok, li:li + 1], in_=SS[:ntok, li:li + 1], func=AF.Sqrt, scale=1.0 / DG,
                    bias=EPSB[:ntok, :])), waits=[sd_free[0]], hard=[tq], sem=s_act)
                sd_free[0] = None
                t2 = P.op("dve", (lambda h, ntok=ntok, li=li: h.reciprocal(out=RS[:ntok, li:li + 1],
                                                                          in_=SD[:ntok, li:li + 1])),
                          waits=[t1], sem=s_dve)
                if st.sample:
                    t5 = P.op("dve", (lambda h, ntok=ntok, li=li: h.scalar_tensor_tensor(
                        out=XN[:ntok, 0:DG], in0=XN[:ntok, 0:DG], scalar=RS[:ntok, li:li + 1], in1=GVBC[:ntok, :],
                        op0=ALU.mult, op1=ALU.mult)), waits=[t1], hard=[t2], sem=s_dve)
                    t_vn[li] = P.op("dve", (lambda h, ntok=ntok, li=li: h.tensor_copy(
                        out=Vn[:ntok, li, :], in_=XN[:ntok, 0:DG])), waits=dst_free, sem=s_dve)
                    t6 = P.op("pool", (lambda h, st=st: h.dma_start(out=gvs[:, :], in_=XN[:st.nreal, 0:DG])),
                              waits=[t5], sem=s_mst, amt=16)
                    xn_free[:] = [t6, t_vn[li]]
                else:
                    t_vn[li] = P.op("dve", (lambda h, ntok=ntok, li=li: h.scalar_tensor_tensor(
                        out=Vn[:ntok, li, :], in0=VnF[:ntok, li, :], scalar=RS[:ntok, li:li + 1], in1=GVBC[:ntok, :],
                        op0=ALU.mult, op1=ALU.mult)), hard=[t2], sem=s_dve)
            t_vall = t_vn[len(mcols) - 1]
            if pi == 0 and si == 0:
                getattr(cfg, "dbg", False) and dump("ss", SS[:, :], [t_vall])
                getattr(cfg, "dbg", False) and dump("sd", SD[:, :], [t_vall])
                getattr(cfg, "dbg", False) and dump("rs", RS[:, :], [t_vall])
                getattr(cfg, "dbg", False) and dump("vn", VnF[:, :, :], [t_vall])
            TC, Z, Y, SQ = TMPR[:, 0, :], TMPR[:, 1, :], TMPR[:, 2, :], SQT[:, :]
            tc_free = z_free = y_free = sq_free = None
            t_sa = None
            for pr in range(DC // UW):
                uc = ring.acquire((mtag, "c", pr))
                ux = ring.acquire((mtag, "x", pr))
                ub = ring.acquire((mtag, "b", pr))
                tC = tX = tB = None
                for cl in range(2):
                    c = 2 * pr + cl
                    bc, rdc = banks.get()
                    tC = pe_job([(lambda h, k=k, bc=bc, cl=cl, uc=uc, Tm=Tm: h.matmul(
                        PS[:, bc, 0:Tm], uc.v[:, k, cl * 128:(cl + 1) * 128], hTm[:, k, 0:Tm],
                        start=(k == 0), stop=(k == KC - 1))) for k in range(KC)], [uc.ld, t_hm, rdc])
                    bx, rdx = banks.get()
                    tX = pe_job([(lambda h, k=k, bx=bx, cl=cl, ux=ux, Tm=Tm: h.matmul(
                        PS[:, bx, 0:Tm], ux.v[:, k, cl * 128:(cl + 1) * 128], hTm[:, k, 0:Tm],
                        start=(k == 0), stop=(k == KC - 1))) for k in range(KC)], [ux.ld, rdx])
                    bb_, rdb = banks.get()
                    tB = pe_job([(lambda h, k=k, bb_=bb_, cl=cl, ub=ub, Tm=Tm: h.matmul(
                        PS[:, bb_, 0:Tm], ub.v[:, k, cl * 128:(cl + 1) * 128], hTm[:, k, 0:Tm],
                        start=(k == 0), stop=(k == KC - 1))) for k in range(KC)], [ub.ld, rdb])
                    t_c = P.op("act", (lambda h, bc=bc, Tm=Tm: h.activation(out=TC[:, 0:Tm], in_=PS[:, bc, 0:Tm],
                                                                      func=AF.Identity)),
                               waits=[tC, tc_free], sem=s_act)
                    banks.set_reader(bc, t_c)
                    t_y = None
                    for sg in segs:
                        c0, n, zo, sq_ = sg["c0"], sg["n"], sg["zo"], sg["seq"]
                        P.op("dve", (lambda h, zo=zo, sq_=sq_, c=c: h.tensor_copy(out=Z[:, zo - 2:zo],
                                                                                   in_=CARRY[:, sq_, c, :])),
                             waits=[z_free, carry_t[sq_]])
                        t_z = P.op("dve", (lambda h, zo=zo, n=n, c0=c0, bx=bx: h.tensor_tensor(
                            out=Z[:, zo:zo + n], in0=TC[:, c0:c0 + n], in1=PS[:, bx, c0:c0 + n], op=ALU.mult)),
                                   waits=[t_c, tX], sem=s_dve)
                        t_cr = P.op("dve", (lambda h, zo=zo, nr=sg["nr"], sq_=sq_, c=c: h.tensor_copy(
                            out=CARRY[:, sq_, c, :], in_=Z[:, zo + nr - 2:zo + nr])), hard=[t_z], sem=s_dve)
                        sg["tcr"] = t_cr
                        P.op("dve", (lambda h, zo=zo, n=n, c0=c0, c=c: h.tensor_scalar_mul(
                            out=Y[:, c0:c0 + n], in0=Z[:, zo - 2:zo - 2 + n], scalar1=CW[:, c, 0:1])),
                             waits=[y_free])
                        P.op("dve", (lambda h, zo=zo, n=n, c0=c0, c=c: h.scalar_tensor_tensor(
                            out=Y[:, c0:c0 + n], in0=Z[:, zo - 1:zo - 1 + n], scalar=CW[:, c, 1:2],
                            in1=Y[:, c0:c0 + n], op0=ALU.mult, op1=ALU.add)))
                        P.op("dve", (lambda h, zo=zo, n=n, c0=c0, c=c: h.scalar_tensor_tensor(
                            out=Y[:, c0:c0 + n], in0=Z[:, zo:zo + n], scalar=CW[:, c, 2:3],
                            in1=Y[:, c0:c0 + n], op0=ALU.mult, op1=ALU.add)))
                        t_y = P.op("dve", (lambda h, n=n, c0=c0, bb_=bb_: h.tensor_tensor(
                            out=Y[:, c0:c0 + n], in0=Y[:, c0:c0 + n], in1=PS[:, bb_, c0:c0 + n], op=ALU.mult)),
                                   waits=[tB], sem=s_dve)
                    banks.set_reader(bx, t_y)
                    banks.set_reader(bb_, t_y)
                    tc_free = t_y
                    z_free = t_y
                    t_sq = P.op("act", (lambda h, Tm=Tm: h.activation(out=SQ[:, 0:Tm], in_=Y[:, 0:Tm], func=AF.Square)),
                                waits=[t_y, sq_free], sem=s_act)
                    t_ya = P.op("act", (lambda h, c=c, Tm=Tm: h.activation(out=yaT[:, c, 0:Tm], in_=Y[:, 0:Tm],
                                                                     func=AF.Identity, scale=GA[:, c:c + 1])),
                                waits=dst_free, sem=s_act)
                    y_free = t_ya
                    for li, (i, ntok, lc) in enumerate(mcols):
                        col = ((0 * MT + li) * NCC + c) * 2
                        sq_free = P.op("pe", (lambda h, ntok=ntok, lc=lc, col=col: h.matmul(
                            PS[:ntok, STATB, col:col + 2], SQ[:, lc:lc + ntok], ONES2[:, :], start=True, stop=True)),
                                       waits=[t_sq, stat_free[0]], sem=s_pe)
                    t_sa = sq_free
                ring.release(uc, tC)
                ring.release(ux, tX)
                ring.release(ub, tB)
            for sg in segs:
                sq_ = sg["seq"]
                carry_t[sq_] = sg["tcr"]
                lst = sg["last"]
                if lst.sample or lst.t == cfg.NT - 1:
                    dstd = css if lst.sample else csp[sq_]
                    for c in range(NCC):
                        t = P.op("pool", (lambda h, c=c, sq_=sq_, dstd=dstd: h.dma_start(
                            out=dstd[:, c * 128:(c + 1) * 128].rearrange("r p -> p r"), in_=CARRY[:, sq_, c, :])),
                                 waits=[sg["tcr"]], sem=s_mst, amt=16)
            t_sb = None
            for pr in range(DG // UW):
                uu_ = ring.acquire((mtag, "uu", pr))
                tUj = None
                for cl in range(2):
                    g = 2 * pr + cl
                    bu, rdu = banks.get()
                    tUj = pe_job([(lambda h, k=k, bu=bu, cl=cl, uu_=uu_, Tm=Tm: h.matmul(
                        PS[:, bu, 0:Tm], uu_.v[:, k, cl * 128:(cl + 1) * 128], hTm[:, k, 0:Tm],
                        start=(k == 0), stop=(k == KC - 1))) for k in range(KC)], [uu_.ld, t_hm, rdu])
                    bm, rdm = banks.get()
                    tMj = pe_job([(lambda h, bm=bm, g=g, li=li, ntok=ntok, lc=lc: h.matmul(
                        PS[:, bm, lc:lc + ntok], Vn[:ntok, li, g * 128:(g + 1) * 128], WMT[:ntok, g, 0:ntok],
                        start=True, stop=True)) for li, (i, ntok, lc) in enumerate(mcols)], [t_vall, rdm, t_w])
                    t_gu = P.op("act", (lambda h, bu=bu, Tm=Tm: h.activation(out=TC[:, 0:Tm], in_=PS[:, bu, 0:Tm],
                                                                       func=AF.Gelu_apprx_tanh)),
                                waits=[tUj, tc_free], sem=s_act)
                    banks.set_reader(bu, t_gu)
                    t_y = None
                    for li, (i, ntok, lc) in enumerate(mcols):
                        P.op("dve", (lambda h, bm=bm, g=g, ntok=ntok, lc=lc: h.tensor_tensor(
                            out=Y[:, lc:lc + ntok], in0=PS[:, bm, lc:lc + ntok], in1=BB[:, g, 0:ntok], op=ALU.add)),
                             waits=[tMj, y_free])
                    t_y = P.op("dve", (lambda h, Tm=Tm: h.tensor_tensor(out=Y[:, 0:Tm], in0=Y[:, 0:Tm], in1=TC[:, 0:Tm],
                                                                  op=ALU.mult)), waits=[t_gu], sem=s_dve)
                    banks.set_reader(bm, t_y)
                    tc_free = t_y
                    t_sq = P.op("act", (lambda h, Tm=Tm: h.activation(out=SQ[:, 0:Tm], in_=Y[:, 0:Tm], func=AF.Square)),
                                waits=[t_y, sq_free], sem=s_act)
                    t_yb = P.op("act", (lambda h, g=g, Tm=Tm: h.activation(out=ybT[:, g, 0:Tm], in_=Y[:, 0:Tm],
                                                                     func=AF.Identity, scale=GB[:, g:g + 1])),
                                waits=dst_free, sem=s_act)
                    y_free = t_yb
                    for li, (i, ntok, lc) in enumerate(mcols):
                        col = ((1 * MT + li) * NCC + g) * 2
                        sq_free = P.op("pe", (lambda h, ntok=ntok, lc=lc, col=col: h.matmul(
                            PS[:ntok, STATB, col:col + 2], SQ[:, lc:lc + ntok], ONES2[:, :], start=True, stop=True)),
                                       waits=[t_sq], sem=s_pe)
                    t_sb = sq_free
                ring.release(uu_, tUj)
            t_r = None
            t_st = None
            for li, (i, ntok, lc) in enumerate(mcols):
                for ab in range(2):
                    c0 = ((ab * MT + li) * NCC) * 2
                    t_st = P.op("dve", (lambda h, ntok=ntok, c0=c0, ab=ab, li=li: h.tensor_reduce(
                        out=ST2[:ntok, ab, li:li + 1],
                        in_=PS[:ntok, STATB, c0:c0 + 2 * NCC].rearrange("p (c t) -> p c t", t=2)[:, :, 0],
                        axis=AX.X, op=ALU.add)), waits=[t_sa, t_sb], sem=s_dve)
            stat_free[0] = t_st
            for li, (i, ntok, lc) in enumerate(mcols):
                t1 = P.op("act", (lambda h, ntok=ntok, li=li: h.activation(
                    out=SD[:ntok, 0:2], in_=ST2[:ntok, :, li], func=AF.Sqrt, scale=1.0 / DC, bias=EPSB[:ntok, :])),
                          waits=[t_st, sd_free[0]], sem=s_act)
                t_r = P.op("dve", (lambda h, ntok=ntok, li=li: h.reciprocal(out=RAB[:ntok, :, li], in_=SD[:ntok, 0:2])),
                           waits=[t1], sem=s_dve)
                sd_free[0] = t_r
            t_xo = None
            last_pe = None
            for oi in range(D // UW):
                uo = ring.acquire((mtag, "o", oi))
                tj = None
                for li, (i, ntok, lc) in enumerate(mcols):
                    for ab, src, nk in ((0, yaT, NCC), (1, ybT, NG)):
                        b, rd = banks.get()
                        koff = 0 if ab == 0 else NCC
                        tj = pe_job([(lambda h, k=k, b=b, ntok=ntok, lc=lc, src=src, koff=koff, uo=uo, nk=nk: h.matmul(
                            PS[:ntok, b, 0:UW], src[:, k, lc:lc + ntok], uo.v[:, koff + k, :],
                            start=(k == 0), stop=(k == nk - 1))) for k in range(nk)], [uo.ld, t_yb, t_ya, rd])
                        t_xo = P.op("dve", (lambda h, b=b, i=i, ntok=ntok, oi=oi, ab=ab, li=li: h.scalar_tensor_tensor(
                            out=X[:ntok, i, oi * UW:(oi + 1) * UW], in0=PS[:ntok, b, 0:UW],
                            scalar=RAB[:ntok, ab, li:li + 1], in1=X[:ntok, i, oi * UW:(oi + 1) * UW],
                            op0=ALU.mult, op1=ALU.add)), waits=[tj, x_nr], hard=[t_r], sem=s_dve)
                        banks.set_reader(b, t_xo)
                ring.release(uo, tj)
                last_pe = tj
            t_x = t_xo
            reg_free = [last_pe]
            if pi == 0 and si == 0:
                getattr(cfg, "dbg", False) and dump("x2", X[:, 0:nsl, :], [t_x])
                getattr(cfg, "dbg", False) and dump("yaT", yaT.bitcast(F32)[:, :, 0:Tm], [t_x])
                getattr(cfg, "dbg", False) and dump("ybT", ybT.bitcast(F32)[:, :, 0:Tm], [t_x])
                getattr(cfg, "dbg", False) and dump("rab", RAB[:, :, :], [t_x])
                getattr(cfg, "dbg", False) and dump("gvbc", GVBC[:, :], [t_x])
                getattr(cfg, "dbg", False) and dump("bb", BB[:, :, :], [t_x])
                getattr(cfg, "dbg", False) and dump("carry", CARRY[:].rearrange("p s c r -> p (s c r)"), [t_x])
        dst_free[:] = list(reg_free)
        xr = {i: [t_x] for i in range(nsl)}
        t_ht = norm_T(cols, G3, hT, xr)
        x_nr = x_last_read[cols[-1][0]]
        t_x, at_free, t_gul = ffn((pi, 2), tiles, cols, Tp, t_ht, x_nr)
        reg_free = [t for t in at_free if t is not None]
        if bf:
            GBCF = REG.bitcast(F32)[:, 0:D]
            GBC = GBCF
            JUNK = REG[:, 2 * D:3 * D]
        else:
            GBC = REG[:, 0:D]
            GBCF = GBC.bitcast(F32)
            JUNK = REG[:, D:2 * D]
        t_gb = P.op("pool", (lambda h: h.dma_start(out=GBC, in_=gfd.partition_broadcast(128))),
                    waits=[t_gul], sem=s_gb, amt=16)
        for (i, ntok, co) in cols:
            st = tiles[i]
            tq = P.op("act", (lambda h, i=i, ntok=ntok: h.activation(out=JUNK[:ntok, :], in_=X[:ntok, i, :],
                                                                      func=AF.Square, accum_out=SS[:ntok, i:i + 1])),
                      waits=[t_x, t_gul], sem=s_act)
            t1 = P.op("act", (lambda h, i=i, ntok=ntok: h.activation(out=SD[:ntok, i:i + 1], in_=SS[:ntok, i:i + 1],
                                                                      func=AF.Sqrt, scale=1.0 / D, bias=EPSB[:ntok, :])),
                      waits=[sd_free[0]], hard=[tq], sem=s_act)
            sd_free[0] = None
            t2 = P.op("dve", (lambda h, i=i, ntok=ntok: h.reciprocal(out=RS[:ntok, i:i + 1], in_=SD[:ntok, i:i + 1])),
                      waits=[t1], sem=s_dve)
            tf = P.op("dve", (lambda h, i=i, ntok=ntok: h.scalar_tensor_tensor(
                out=XN[:ntok, :], in0=X[:ntok, i, :], scalar=RS[:ntok, i:i + 1], in1=GBCF[:ntok, :],
                op0=ALU.mult, op1=ALU.mult)), waits=[t_gb] + xn_free, hard=[t2], sem=s_dve)
            x_free[i] = tf
            dstd = ys[:, :] if st.sample else yp[st.seq, st.t * 128:(st.t + 1) * 128, :]
            ts = P.op("pool", (lambda h, st=st, dstd=dstd: h.dma_start(out=dstd, in_=XN[:st.nreal, :])),
                      waits=[tf], sem=s_yst, amt=16)
            xn_free[:] = [ts]
    P.op("pool", lambda h: None, waits=[(s_yst, s_yst.n), (s_mst, s_mst.n), (s_dbg, s_dbg.n)])
    assert ring.next_use == len(useq)

    with nc.Block() as block:
        @block.sync
        def _(h):
            emit(h, P.q["sync"])

        @block.scalar
        def _(h):
            emit(h, P.q["act"])

        @block.vector
        def _(h):
            emit(h, P.q["dve"])

        @block.gpsimd
        def _(h):
            with nc.allow_non_contiguous_dma(reason="tiny constant / state layouts"):
                emit(h, P.q["pool"])

        @block.tensor
        def _(h):
            emit(h, P.q["pe"])
    for cm in sems.values():
        cm.__exit__(None, None, None)
    return nc


stat_free = [None]
sd_free = [None]


def make_in_maps(cfg, inputs, ncores):
    f = lambda a: np.ascontiguousarray(np.asarray(a, dtype=np.float32))
    shared = {
        "wg1": f(inputs["ffn1_w_gate"][0]), "wu1": f(inputs["ffn1_w_up"][0]), "wd1": f(inputs["ffn1_w_down"][0]),
        "win": f(inputs["w_in"][0]), "wo": f(inputs["w_o"][0]),
        "wg2": f(inputs["ffn2_w_gate"][0]), "wu2": f(inputs["ffn2_w_up"][0]), "wd2": f(inputs["ffn2_w_down"][0]),
        "g1": f(inputs["ffn1_norm"][0]), "g2": f(inputs["mix_norm"][0]), "g3": f(inputs["ffn2_norm"][0]),
        "gf": f(inputs["final_norm"]).reshape(-1),
        "cw": f(inputs["conv_w"][0]), "gv": f(inputs["gmlp_v_norm"][0]), "ws": f(inputs["gmlp_w_s"][0]),
        "bs": f(inputs["gmlp_b"][0]), "ga": f(inputs["conv_out_norm"][0]), "gb": f(inputs["gmlp_out_norm"][0]),
    }
    xp, xs, cc = f(inputs["x_prompt"]), f(inputs["x_sample"]), f(inputs["cache_conv"][0])
    maps = []
    for c in range(ncores):
        m = dict(shared)
        m["xp"] = xp[c * cfg.NSEQ:(c + 1) * cfg.NSEQ]
        m["xs"] = xs[c]
        m["cc"] = cc[c]
        maps.append(m)
    return maps


_CACHE = {}


def kernel(**inputs):
    cfg = Cfg(fast=FAST, PT=9, MT=4, R=5) if BF16_MM else Cfg(fast=FAST)
    ncores = 8
    if "nc" not in _CACHE:
        stat_free[0] = None
        sd_free[0] = None
        _CACHE["nc"] = build_program(cfg)
    nc = _CACHE["nc"]
    maps = make_in_maps(cfg, inputs, ncores)
    res = run_bass_kernel_spmd(nc, maps, core_ids=list(range(ncores)))
    r = res.results
    y_prompt = np.concatenate([r[c]["yp"] for c in range(ncores)], axis=0)
    y_sample = np.stack([r[c]["ys"] for c in range(ncores)], axis=0)
    csp = np.concatenate([r[c]["csp"] for c in range(ncores)], axis=0)[None]
    css = np.stack([r[c]["css"] for c in range(ncores)], axis=0)[None]
    gvs = np.stack([r[c]["gvs"] for c in range(ncores)], axis=0)[None]
    return (y_prompt.astype(np.float32), y_sample.astype(np.float32), csp.astype(np.float32),
            css.astype(np.float32), gvs.astype(np.float32))
```

```python
import numpy as np
import concourse.bass as bass
import concourse.mybir as mybir
from concourse.bass_utils import run_bass_kernel_spmd

F32 = mybir.dt.float32
F32R = mybir.dt.float32r
BF16 = mybir.dt.bfloat16
AF = mybir.ActivationFunctionType
ALU = mybir.AluOpType
AX = mybir.AxisListType

EPS = 1e-6
FAST = True
BF16_MM = True


class Cfg:
    def __init__(self, D=2048, DFF=5632, NSEQ=2, SEQ=2048, DEC=32, fast=True, PT=6, MT=3, R=4):
        self.D, self.DFF, self.NSEQ, self.SEQ, self.DEC = D, DFF, NSEQ, SEQ, DEC
        self.fast = fast
        self.KC = D // 128
        self.DC = D // 2
        self.DG = D // 2
        self.NCC = self.DC // 128
        self.NG = self.DG // 128
        self.DIN = 3 * self.DC + 2 * self.DG
        self.NT = SEQ // 128
        self.PT = PT
        self.MT = MT
        self.R = R
        self.UW = 256
        self.NCOL = min(512, D)
        assert DFF % 256 == 0 and self.DC % 256 == 0 and D % 256 == 0


class SubTile:
    def __init__(self, seq, t, ntok, sample, nreal=None):
        self.seq, self.t, self.ntok, self.sample = seq, t, ntok, sample
        self.nreal = ntok if nreal is None else nreal


def make_passes(cfg):
    passes = []
    for s in range(cfg.NSEQ):
        nt = cfg.NT
        npass = -(-nt // cfg.PT)
        if s == cfg.NSEQ - 1 and nt % cfg.PT == 0:
            pass
        base = nt // npass
        extra = nt % npass
        sizes = [base + (1 if i < extra else 0) for i in range(npass)]
        t0 = 0
        for sz in sizes:
            passes.append([SubTile(s, t0 + i, 128, False) for i in range(sz)])
            t0 += sz
    smp = SubTile(cfg.NSEQ, 0, 128, True, nreal=cfg.DEC)
    if len(passes[-1]) < cfg.PT:
        passes[-1].append(smp)
    else:
        passes.append([smp])
    return passes


def split_sub(cfg, tiles):
    n = len(tiles)
    ng = -(-n // cfg.MT)
    out, i = [], 0
    base, extra = n // ng, n % ng
    for g in range(ng):
        sz = base + (1 if g < extra else 0)
        out.append(list(range(i, i + sz)))
        i += sz
    return out


def col_chunks(total, maxw=512):
    n = -(-total // maxw)
    w = -(-total // n)
    if w % 2:
        w += 1
    out, c = [], 0
    while c < total:
        out.append((c, min(total, c + w)))
        c += w
    return out


class SemC:
    def __init__(self, h, name):
        self.h, self.n, self.name = h, 0, name


ENGS = ["sync", "act", "dve", "pool", "pe"]


class Plan:
    def __init__(self):
        self.q = {e: [] for e in ENGS}
        self.seen = {e: {} for e in ENGS}
        self.own = {}
        self.block = []

    def op(self, eng, fn, waits=(), sem=None, amt=1, hard=()):
        q = self.q[eng]
        if self.block:
            waits = list(waits) + self.block
        waits = [(t, False) for t in waits] + [(t, True) for t in hard]
        for t, is_hard in waits:
            if t is None:
                continue
            s, v = t
            if s is self.own.get(eng) and not is_hard:
                continue
            if self.seen[eng].get(id(s), 0) >= v:
                continue
            self.seen[eng][id(s)] = v
            q.append((0, s, v))
        if sem is not None:
            sem.n += amt
            q.append((1, fn, sem, amt))
            return (sem, sem.n)
        q.append((1, fn, None, 0))
        return None


def emit(h, items):
    for it in items:
        if it[0] == 0:
            h.wait_ge(it[1].h, it[2])
        else:
            ins = it[1](h)
            if it[2] is not None:
                ins.then_inc(it[2].h, it[3])


class Unit:
    pass


class Ring:
    def __init__(self, K, seq):
        self.K, self.seq = K, seq
        self.next_load, self.next_use, self.released = 0, 0, 0
        self.rel = {}

    def plan_loads(self):
        K = self.K
        R = K.cfg.R
        while self.next_load < len(self.seq) and self.next_load < self.released + R:
            u = self.next_load
            r = u % R
            key, src, kd, n = self.seq[u]
            dst = K.RING[:, r, 0:kd * n].rearrange("p (k n) -> p k n", k=kd)
            waits = [self.rel[u - R]] if u >= R else []
            K.P.op(K.ring_eng, (lambda h, dst=dst, src=src: h.dma_start(out=dst, in_=src)),
                   waits=waits, sem=K.slot_ld[r], amt=16)
            self.next_load += 1

    def acquire(self, key):
        K = self.K
        u = self.next_use
        self.next_use += 1
        k2, src, kd, n = self.seq[u]
        assert k2 == key, (k2, key)
        assert u < self.next_load, "unit load not planned yet (ring too small for this schedule)"
        r = u % K.cfg.R
        un = Unit()
        un.idx = u
        un.v = K.RING[:, r, 0:kd * n].rearrange("p (k n) -> p k n", k=kd)
        un.ld = (K.slot_ld[r], 16 * (u // K.cfg.R + 1))
        return un

    def release(self, un, ticket):
        assert un.idx == self.released, (un.idx, self.released)
        self.rel[un.idx] = ticket
        self.released += 1
        self.plan_loads()


class Banks:
    def __init__(self, K, nb):
        self.K, self.nb, self.i = K, nb, 0
        self.reader = [None] * nb

    def get(self):
        b = self.i % self.nb
        self.i += 1
        return b, self.reader[b]

    def set_reader(self, b, ticket):
        self.reader[b] = ticket


class Ctx:
    pass


def build_program(cfg):
    nc = bass.Bass("TRN2", target_bir_lowering=False)
    nc.dge_precook = False
    bf = getattr(cfg, "bf16", BF16_MM)
    MM = BF16 if bf else (F32R if cfg.fast else F32)
    WDT = F32 if bf else MM
    D, DFF, KC, DC, DG, NCC, NG = cfg.D, cfg.DFF, cfg.KC, cfg.DC, cfg.DG, cfg.NCC, cfg.NG
    K = Ctx()
    K.cfg, K.nc = cfg, nc
    K.ring_eng = "pool" if getattr(cfg, "bf16", BF16_MM) else "sync"
    P = K.P = Plan()

    def din(name, shape, dt=F32):
        return nc.dram_tensor(name, list(shape), dt, kind="ExternalInput").ap()

    def dout(name, shape):
        return nc.dram_tensor(name, list(shape), F32, kind="ExternalOutput").ap()

    xp = din("xp", [cfg.NSEQ, cfg.SEQ, D])
    xs = din("xs", [cfg.DEC, D])
    cc = din("cc", [2, DC])
    wts = {}
    for nm, shp in [("wg1", [D, DFF]), ("wu1", [D, DFF]), ("wd1", [DFF, D]), ("win", [D, cfg.DIN]),
                    ("wo", [D, D]), ("wg2", [D, DFF]), ("wu2", [D, DFF]), ("wd2", [DFF, D])]:
        wts[nm] = din(nm, shp, WDT)
    g1d, g2d, g3d, gfd = din("g1", [D]), din("g2", [D]), din("g3", [D]), din("gf", [D], WDT)
    cwd = din("cw", [3, DC])
    gvd = din("gv", [DG])
    wsd = din("ws", [NG, 128, 128])
    bsd = din("bs", [NG, 128])
    gad, gbd = din("ga", [DC]), din("gb", [DG])
    yp = dout("yp", [cfg.NSEQ, cfg.SEQ, D])
    ys = dout("ys", [cfg.DEC, D])
    csp = dout("csp", [cfg.NSEQ, 2, DC])
    css = dout("css", [2, DC])
    gvs = dout("gvs", [cfg.DEC, DG])

    passes = make_passes(cfg)
    PT, MT = cfg.PT, cfg.MT
    TPMAX = PT * 128
    TM = MT * 128

    A = nc.alloc_sbuf_tensor
    X = A("X", [128, PT, D], F32)
    regsz = max(KC * TPMAX + 8 * TPMAX, KC * TM + MT * DG + NCC * TM + NG * TM)
    REG = A("REG", [128, regsz], MM)
    XN = A("XN", [128, D], F32)
    TW = TM + 8
    TMPR = A("TMPR", [128, 3, TW], F32)
    SQT = A("SQT", [128, TW], MM)
    RING = K.RING = A("RING", [128, cfg.R, 16 * 256], MM)
    IDENT = A("IDENT", [128, 128], F32)
    ONES2 = A("ONES2", [128, 2], MM)
    G1, G2, G3 = A("G1", [128, KC], F32), A("G2", [128, KC], F32), A("G3", [128, KC], F32)
    GVBC = A("GVBC", [128, DG], F32)
    BB = A("BB", [128, NG, 128], F32)
    CW = A("CW", [128, NCC, 3], F32)
    GA, GB = A("GA", [128, NCC], F32), A("GB", [128, NG], F32)
    WMT = A("WMT", [128, NG, 128], MM)
    CARRY = A("CARRY", [128, cfg.NSEQ + 1, NCC, 2], F32)
    SS = A("SS", [128, PT], F32)
    SD = A("SD", [128, PT], F32)
    RS = A("RS", [128, PT], F32)
    ST2 = A("ST2", [128, 2, MT], F32)
    RAB = A("RAB", [128, 2, MT], F32)
    PS = nc.alloc_psum_tensor("PS", [128, 8, 512], F32)

    hT = REG[:, 0:KC * TPMAX].rearrange("p (k t) -> p k t", k=KC)
    aT = [REG[:, KC * TPMAX + i * 4 * TPMAX: KC * TPMAX + (i + 1) * 4 * TPMAX].rearrange("p (c t) -> p c t", c=4)
          for i in range(2)]
    o = 0
    hTm = REG[:, o:o + KC * TM].rearrange("p (k t) -> p k t", k=KC); o += KC * TM
    Vn = REG[:, o:o + MT * DG].rearrange("p (s d) -> p s d", s=MT); o += MT * DG
    yaT = REG[:, o:o + NCC * TM].rearrange("p (c t) -> p c t", c=NCC); o += NCC * TM
    ybT = REG[:, o:o + NG * TM].rearrange("p (c t) -> p c t", c=NG); o += NG * TM
    VnF = Vn if bf else Vn.bitcast(F32)

    sems = {}

    def mk(name):
        cm = nc.semaphore(name)
        h = cm.__enter__()
        sems[name] = cm
        return SemC(h, name)

    s_pe, s_act, s_dve, s_pool = mk("s_pe"), mk("s_act"), mk("s_dve"), mk("s_pool")
    P.own = {"pe": s_pe, "act": s_act, "dve": s_dve, "pool": s_pool}
    K.slot_ld = [mk(f"sl{r}") for r in range(cfg.R)]
    xld = [mk(f"xld{i}") for i in range(PT)]
    s_cst = mk("s_cst")
    s_yst = mk("s_yst")
    s_mst = mk("s_mst")
    s_dbg = mk("s_dbg")
    s_gb = mk("s_gb")

    UW = cfg.UW
    NGRP = DFF // UW

    def wview(w, c0, n):
        return w.rearrange("(k p) n -> p k n", p=128)[:, :, c0:c0 + n]

    def ffn_units(tag, wg, wu, wd):
        seq = []

        def gu(j):
            seq.append(((tag, "g", j), wview(wg, j * UW, UW), KC, UW))
            seq.append(((tag, "u", j), wview(wu, j * UW, UW), KC, UW))

        def dn(j):
            seq.append(((tag, "d", j), wd.rearrange("(c p) n -> p c n", p=128)[:, 2 * j:2 * j + 2, :], 2, D))

        assert NGRP % 2 == 0
        gu(0)
        gu(1)
        for J in range(NGRP // 2):
            if 2 * J + 2 < NGRP:
                gu(2 * J + 2)
                gu(2 * J + 3)
            dn(2 * J)
            dn(2 * J + 1)
        return seq

    def mixer_units(tag):
        seq = []
        win, wo = wts["win"], wts["wo"]
        for i in range(DG // UW):
            seq.append(((tag, "v", i), wview(win, 3 * DC + DG + i * UW, UW), KC, UW))
        for i in range(DC // UW):
            seq.append(((tag, "c", i), wview(win, DC + i * UW, UW), KC, UW))
            seq.append(((tag, "x", i), wview(win, 2 * DC + i * UW, UW), KC, UW))
            seq.append(((tag, "b", i), wview(win, i * UW, UW), KC, UW))
        for i in range(DG // UW):
            seq.append(((tag, "uu", i), wview(win, 3 * DC + i * UW, UW), KC, UW))
        for i in range(D // UW):
            seq.append(((tag, "o", i), wview(wo, i * UW, UW), KC, UW))
        return seq

    useq = []
    for pi, tiles in enumerate(passes):
        useq += ffn_units((pi, 1), wts["wg1"], wts["wu1"], wts["wd1"])
        for si, _ in enumerate(split_sub(cfg, tiles)):
            useq += mixer_units((pi, "m", si))
        useq += ffn_units((pi, 2), wts["wg2"], wts["wu2"], wts["wd2"])
    ring = K.ring = Ring(K, useq)
    ring.plan_loads()
    banks = Banks(K, 7)
    STATB = 7

    def dma_pool(out, in_, sem):
        return P.op("pool", (lambda h: h.dma_start(out=out, in_=in_)), sem=sem, amt=16)

    K.dumps = {}

    def dump(name, ap, waits):
        if not getattr(cfg, "dbg", False):
            return
        shp = list(ap.shape)
        d = nc.dram_tensor("dbg_" + name, shp, F32, kind="ExternalOutput").ap()
        t = P.op("pool", (lambda h: h.dma_start(out=d, in_=ap)), waits=waits, sem=s_dbg, amt=16)
        P.block.append(t)

    P.op("pool", lambda h: h.memset(IDENT[:], 0.0))
    t_ident = P.op("pool", lambda h: h.affine_select(out=IDENT[:], in_=IDENT[:], compare_op=ALU.not_equal, fill=1.0,
                                                     base=0, pattern=[[-1, 128]], channel_multiplier=1), sem=s_pool)
    ONESF = A("ONESF", [128, 64], F32)
    ZT = A("ZT", [128, 64], F32)
    P.op("pool", lambda h: h.memset(ONESF[:], 1.0))
    P.op("pool", lambda h: h.memset(ZT[:], 0.0))
    P.op("pool", lambda h: h.tensor_copy(out=ONES2[:], in_=ONESF[:, 0:2]))
    t_cz = P.op("pool", lambda h: h.memset(CARRY[:].rearrange("p s c r -> p (s c r)"), 0.0), sem=s_pool)
    for Gt, gd in ((G1, g1d), (G2, g2d), (G3, g3d)):
        for k in range(KC):
            dma_pool(Gt[:, k:k + 1], gd[k * 128:(k + 1) * 128].rearrange("(p o) -> p o", o=1), s_cst)
    for c in range(NCC):
        dma_pool(GA[:, c:c + 1], gad[c * 128:(c + 1) * 128].rearrange("(p o) -> p o", o=1), s_cst)
        for r in range(3):
            dma_pool(CW[:, c, r:r + 1], cwd[r, c * 128:(c + 1) * 128].rearrange("(p o) -> p o", o=1), s_cst)
    for c in range(NG):
        dma_pool(GB[:, c:c + 1], gbd[c * 128:(c + 1) * 128].rearrange("(p o) -> p o", o=1), s_cst)
    dma_pool(GVBC[:], gvd.partition_broadcast(128), s_cst)
    dma_pool(BB[:], bsd.partition_broadcast(128), s_cst)
    WSN = XN[:, 0:NG * 128].rearrange("p (g j) -> p g j", g=NG)
    dma_pool(WSN, wsd.rearrange("g i j -> i g j"), s_cst)
    for c in range(NCC):
        P.op("pool", (lambda h, c=c: h.dma_start(out=CARRY[:, cfg.NSEQ, c, :],
                                                  in_=cc[:, c * 128:(c + 1) * 128].rearrange("r p -> p r"))),
             waits=[t_cz], sem=s_cst, amt=16)
    t_cst = (s_cst, s_cst.n)
    t_w = None
    for g in range(NG):
        b, rd = banks.get()
        tj = P.op("pe", (lambda h, b=b, g=g: h.transpose(out=PS[:, b, 0:128], in_=WSN[:, g, :], identity=IDENT[:])),
                  waits=[t_cst, t_ident, rd], sem=s_pe)
        t_w = P.op("dve", (lambda h, b=b, g=g: h.tensor_copy(out=WMT[:, g, :], in_=PS[:, b, 0:128])),
                   waits=[tj], sem=s_dve)
        banks.set_reader(b, t_w)
    t_w = P.op("dve", lambda h: h.tensor_copy(out=WMT[64:128, :, 0:64],
                                               in_=ZT[64:128, :].unsqueeze(1).to_broadcast([64, NG, 64])),
               waits=[t_cz], sem=s_dve)
    xn_free = [t_w]

    x_free = [None] * PT
    carry_t = [t_cst] * (cfg.NSEQ + 1)

    def pe_job(mms, waits):
        t = None
        for i, fn in enumerate(mms):
            last = i == len(mms) - 1
            t = P.op("pe", fn, waits=waits if i == 0 else (), sem=s_pe if last else None)
        return t

    def norm_T(slots, Gt, dst, x_ready):
        last = None
        for (i, ntok, co) in slots:
            tq = P.op("act", (lambda h, i=i, ntok=ntok: h.activation(out=XN[:ntok, :], in_=X[:ntok, i, :],
                                                                      func=AF.Square, accum_out=SS[:ntok, i:i + 1])),
                      waits=list(x_ready[i]) + xn_free + [t_eps], sem=s_act)
            t1 = P.op("act", (lambda h, i=i, ntok=ntok: h.activation(out=SD[:ntok, i:i + 1], in_=SS[:ntok, i:i + 1],
                                                                      func=AF.Sqrt, scale=1.0 / D, bias=EPSB[:ntok, :])),
                      waits=[sd_free[0]], hard=[tq], sem=s_act)
            sd_free[0] = None
            t2 = P.op("dve", (lambda h, i=i, ntok=ntok: h.reciprocal(out=RS[:ntok, i:i + 1], in_=SD[:ntok, i:i + 1])),
                      waits=[t1], sem=s_dve)
            t3 = P.op("act", (lambda h, i=i, ntok=ntok: h.activation(out=XN[:ntok, :], in_=X[:ntok, i, :],
                                                                      func=AF.Identity, scale=RS[:ntok, i:i + 1])),
                      waits=[t2], sem=s_act)
            x_last_read[i] = t3
            tj = None
            for b0 in range(0, KC, 4):
                nq = min(4, KC - b0)
                b, rd = banks.get()
                mms = [(lambda h, b=b, q=q, k=b0 + q, ntok=ntok: h.transpose(
                    out=PS[:, b, q * 128:q * 128 + ntok], in_=XN[:ntok, k * 128:(k + 1) * 128],
                    identity=IDENT[:ntok, :ntok])) for q in range(nq)]
                tj = pe_job(mms, [t3, rd, t_ident])
                bv = PS[:, b, 0:nq * 128].rearrange("p (q t) -> p q t", q=nq)[:, :, 0:ntok]
                gv_ = Gt[:, b0:b0 + nq].unsqueeze(2).to_broadcast([128, nq, ntok])
                last = P.op("dve", (lambda h, bv=bv, gv_=gv_, b0=b0, nq=nq, co=co, ntok=ntok: h.tensor_tensor(
                    out=dst[:, b0:b0 + nq, co:co + ntok], in0=bv, in1=gv_, op=ALU.mult)),
                            waits=[tj, t_cst] + dst_free, sem=s_dve)
                banks.set_reader(b, last)
            xn_free[:] = [tj]
        return last

    EPSB = A("EPSB", [128, 1], F32)
    P.op("pool", lambda h: h.memset(EPSB[:], EPS))
    t_eps = P.op("pool", lambda h: h.memset(SS[:], 0.0), sem=s_pool)
    x_last_read = [None] * PT
    dst_free = []

    def ffn(tag, tiles, cols, Tp, ht_ticket, x_norm_read):
        halves = col_chunks(Tp)
        at_ready = {}
        at_ready_g = {}
        at_free = [None, None]
        tmp_free = [None, None]
        ti = [0]
        last_acc = [None]
        last_gu = [None]

        def GU(j):
            ug = ring.acquire((tag, "g", j))
            uu = ring.acquire((tag, "u", j))
            tG = tU = tM = None
            for c in range(2):
                for (h0, h1) in halves:
                    n = h1 - h0
                    bg, rdg = banks.get()
                    tG = pe_job([(lambda h, k=k, c=c, bg=bg, h0=h0, h1=h1, n=n: h.matmul(
                        PS[:, bg, 0:n], ug.v[:, k, c * 128:(c + 1) * 128], hT[:, k, h0:h1],
                        start=(k == 0), stop=(k == KC - 1))) for k in range(KC)], [ug.ld, ht_ticket, rdg])
                    bu, rdu = banks.get()
                    tU = pe_job([(lambda h, k=k, c=c, bu=bu, h0=h0, h1=h1, n=n: h.matmul(
                        PS[:, bu, 0:n], uu.v[:, k, c * 128:(c + 1) * 128], hT[:, k, h0:h1],
                        start=(k == 0), stop=(k == KC - 1))) for k in range(KC)], [uu.ld, ht_ticket, rdu])
                    tb = ti[0] % 2
                    ti[0] += 1
                    tS = P.op("act", (lambda h, tb=tb, bg=bg, n=n: h.activation(
                        out=TMPR[:, tb, 0:n], in_=PS[:, bg, 0:n], func=AF.Silu)),
                              waits=[tG, tmp_free[tb]], sem=s_act)
                    banks.set_reader(bg, tS)
                    tM = P.op("dve", (lambda h, tb=tb, bu=bu, n=n, c=c, h0=h0, h1=h1, j=j: h.tensor_tensor(
                        out=aT[(j // 2) % 2][:, (j % 2) * 2 + c, h0:h1], in0=TMPR[:, tb, 0:n], in1=PS[:, bu, 0:n],
                        op=ALU.mult)),
                              waits=[tS, tU, at_free[(j // 2) % 2]], sem=s_dve)
                    banks.set_reader(bu, tM)
                    tmp_free[tb] = tM
            ring.release(ug, tG)
            ring.release(uu, tU)
            at_ready_g[j] = tM
            last_gu[0] = tU

        def DN2(J):
            uds = [ring.acquire((tag, "d", 2 * J)), ring.acquire((tag, "d", 2 * J + 1))]
            tD = None
            NC_ = cfg.NCOL
            def dn_job(ntok, co, n0):
                b, rd = banks.get()
                t = pe_job([(lambda h, c4=c4, b=b: h.matmul(
                    PS[:ntok, b, 0:NC_], aT[J % 2][:, c4, co:co + ntok], uds[c4 // 2].v[:, c4 % 2, n0:n0 + NC_],
                    start=(c4 == 0), stop=(c4 == 3))) for c4 in range(4)],
                           [uds[0].ld, uds[1].ld, at_ready[J], rd])
                return b, t

            for (i, ntok, co) in cols:
                n0s = list(range(0, D, NC_))
                q = 0
                while q < len(n0s):
                    n0 = n0s[q]
                    b, tD = dn_job(ntok, co, n0)
                    if q + 1 < len(n0s) and b + 1 < banks.nb:
                        b2, tD = dn_job(ntok, co, n0s[q + 1])
                        assert b2 == b + 1
                        tA = P.op("dve", (lambda h, b=b, i=i, ntok=ntok, n0=n0: h.scalar_tensor_tensor(
                            out=X[:ntok, i, n0:n0 + 2 * NC_].rearrange("p (a n) -> p a n", a=2),
                            in0=PS[:ntok, b:b + 2, 0:NC_], scalar=0.5,
                            in1=X[:ntok, i, n0:n0 + 2 * NC_].rearrange("p (a n) -> p a n", a=2),
                            op0=ALU.mult, op1=ALU.add)), waits=[tD, x_norm_read], sem=s_dve)
                        banks.set_reader(b, tA)
                        banks.set_reader(b2, tA)
                        q += 2
                    else:
                        tA = P.op("dve", (lambda h, b=b, i=i, ntok=ntok, n0=n0: h.scalar_tensor_tensor(
                            out=X[:ntok, i, n0:n0 + NC_], in0=PS[:ntok, b, 0:NC_], scalar=0.5,
                            in1=X[:ntok, i, n0:n0 + NC_], op0=ALU.mult, op1=ALU.add)),
                                  waits=[tD, x_norm_read], sem=s_dve)
                        banks.set_reader(b, tA)
                        q += 1
                    last_acc[0] = tA
            ring.release(uds[0], tD)
            ring.release(uds[1], tD)
            at_free[J % 2] = tD

        GU(0)
        GU(1)
        at_ready2 = {}
        for J in range(NGRP // 2):
            at_ready[J] = at_ready_g[2 * J + 1]
            if 2 * J + 2 < NGRP:
                GU(2 * J + 2)
                GU(2 * J + 3)
            DN2(J)
        return last_acc[0], at_free, last_gu[0]

    out_tickets = []
    reg_free = []
    for pi, tiles in enumerate(passes):
        nsl = len(tiles)
        cols = [(i, st.ntok, 128 * i) for i, st in enumerate(tiles)]
        Tp = sum(st.ntok for st in tiles)
        x_ready = {}
        for i, st in enumerate(tiles):
            src = xs[:, :] if st.sample else xp[st.seq, st.t * 128:(st.t + 1) * 128, :]
            t = P.op("pool", (lambda h, i=i, st=st, src=src: h.dma_start(out=X[:st.nreal, i, :], in_=src)),
                     waits=[x_free[i]], sem=xld[i], amt=16)
            x_ready[i] = [t]
            if st.nreal < st.ntok:
                assert st.nreal == 32
                P.op("dve", (lambda h, i=i: h.memset(X[32:64, i, :], 0.0)), waits=[x_free[i]])
                tz = P.op("dve", (lambda h, i=i: h.memset(X[64:128, i, :], 0.0)), sem=s_dve)
                x_ready[i].append(tz)
        dst_free[:] = list(reg_free)
        t_ht = norm_T(cols, G1, hT, x_ready)
        if pi == 0:
            getattr(cfg, "dbg", False) and dump("hT1", hT.bitcast(F32)[:, :, 0:Tp], [t_ht])
        x_nr = x_last_read[cols[-1][0]]
        t_x, at_free, _ = ffn((pi, 1), tiles, cols, Tp, t_ht, x_nr)
        if pi == 0:
            getattr(cfg, "dbg", False) and dump("x1", X[:, 0:nsl, :], [t_x])
        reg_free = [t for t in at_free if t is not None]
        for si, idxs in enumerate(split_sub(cfg, tiles)):
            mtag = (pi, "m", si)
            sub = [tiles[i] for i in idxs]
            mcols = [(i, tiles[i].ntok, 128 * li) for li, i in enumerate(idxs)]
            Tm = sum(tiles[i].ntok for i in idxs)
            segs = []
            for li, i in enumerate(idxs):
                st = tiles[i]
                if segs and segs[-1]["seq"] == st.seq:
                    segs[-1]["n"] += st.ntok
                    segs[-1]["nr"] += st.nreal
                    segs[-1]["last"] = st
                else:
                    segs.append({"seq": st.seq, "c0": 128 * li, "n": st.ntok, "nr": st.nreal, "zo": 128 * li + 2 * len(segs) + 2,
                                 "first": st, "last": st})
            dst_free[:] = list(reg_free)
            xr = {i: [t_x] for i in idxs}
            t_hm = norm_T(mcols, G2, hTm, xr)
            if pi == 0 and si == 0:
                getattr(cfg, "dbg", False) and dump("hTm", hTm.bitcast(F32)[:, :, 0:Tm], [t_hm])
            x_nr = x_last_read[idxs[-1]]
            t_vn = {}
            for vi in range(DG // UW):
                un = ring.acquire((mtag, "v", vi))
                tj = None
                for li, (i, ntok, lc) in enumerate(mcols):
                    b, rd = banks.get()
                    tj = pe_job([(lambda h, k=k, b=b, ntok=ntok, lc=lc, un=un: h.matmul(
                        PS[:ntok, b, 0:UW], hTm[:, k, lc:lc + ntok], un.v[:, k, :],
                        start=(k == 0), stop=(k == KC - 1))) for k in range(KC)], [un.ld, t_hm, rd])
                    st = tiles[i]
                    if st.sample:
                        tg = P.op("act", (lambda h, b=b, ntok=ntok, vi=vi: h.activation(
                            out=XN[:ntok, vi * UW:(vi + 1) * UW], in_=PS[:ntok, b, 0:UW], func=AF.Gelu_apprx_tanh)),
                                  waits=[tj] + xn_free, sem=s_act)
                    else:
                        tg = P.op("act", (lambda h, b=b, ntok=ntok, li=li, vi=vi: h.activation(
                            out=Vn[:ntok, li, vi * UW:(vi + 1) * UW], in_=PS[:ntok, b, 0:UW], func=AF.Gelu_apprx_tanh)),
                                  waits=[tj] + dst_free, sem=s_act)
                    banks.set_reader(b, tg)
                ring.release(un, tj)
            for li, (i, ntok, lc) in enumerate(mcols):
                st = tiles[i]
                src = XN[:ntok, 0:DG] if st.sample else VnF[:ntok, li, :]
                junk = TMPR[:ntok].rearrange("p a b -> p (a b)")[:, 0:DG]
                tq = P.op("act", (lambda h, ntok=ntok, li=li, src=src, junk=junk: h.activation(
                    out=junk, in_=src, func=AF.Square, accum_out=SS[:ntok, li:li + 1])),
                          waits=xn_free, sem=s_act)
                t1 = P.op("act", (lambda h, ntok=ntok, li=li: h.activation(
                    out=SD[:ntok, li:li + 1], in_=SS[:ntok, li:li + 1], func=AF.Sqrt, scale=1.0 / DG,
                    bias=EPSB[:ntok, :])), waits=[sd_free[0]], hard=[tq], sem=s_act)
                sd_free[0] = None
                t2 = P.op("dve", (lambda h, ntok=ntok, li=li: h.reciprocal(out=RS[:ntok, li:li + 1],
                                                                          in_=SD[:ntok, li:li + 1])),
                          waits=[t1], sem=s_dve)
                if st.sample:
                    t5 = P.op("dve", (lambda h, ntok=ntok, li=li: h.scalar_tensor_tensor(
                        out=XN[:ntok, 0:DG], in0=XN[:ntok, 0:DG], scalar=RS[:ntok, li:li + 1], in1=GVBC[:ntok, :],
                        op0=ALU.mult, op1=ALU.mult)), waits=[t1], hard=[t2], sem=s_dve)
                    t_vn[li] = P.op("dve", (lambda h, ntok=ntok, li=li: h.tensor_copy(
                        out=Vn[:ntok, li, :], in_=XN[:ntok, 0:DG])), waits=dst_free, sem=s_dve)
                    t6 = P.op("pool", (lambda h, st=st: h.dma_start(out=gvs[:, :], in_=XN[:st.nreal, 0:DG])),
                              waits=[t5], sem=s_mst, amt=16)
                    xn_free[:] = [t6, t_vn[li]]
                else:
                    t_vn[li] = P.op("dve", (lambda h, ntok=ntok, li=li: h.scalar_tensor_tensor(
                        out=Vn[:ntok, li, :], in0=VnF[:ntok, li, :], scalar=RS[:ntok, li:li + 1], in1=GVBC[:ntok, :],
                        op0=ALU.mult, op1=ALU.mult)), hard=[t2], sem=s_dve)
            t_vall = t_vn[len(mcols) - 1]
            if pi == 0 and si == 0:
                getattr(cfg, "dbg", False) and dump("ss", SS[:, :], [t_vall])
                getattr(cfg, "dbg", False) and dump("sd", SD[:, :], [t_vall])
                getattr(cfg, "dbg", False) and dump("rs", RS[:, :], [t_vall])
                getattr(cfg, "dbg", False) and dump("vn", VnF[:, :, :], [t_vall])
            TC, Z, Y, SQ = TMPR[:, 0, :], TMPR[:, 1, :], TMPR[:, 2, :], SQT[:, :]
            tc_free = z_free = y_free = sq_free = None
            t_sa = None
            for pr in range(DC // UW):
                uc = ring.acquire((mtag, "c", pr))
                ux = ring.acquire((mtag, "x", pr))
                ub = ring.acquire((mtag, "b", pr))
                tC = tX = tB = None
                for cl in range(2):
                    c = 2 * pr + cl
                    bc, rdc = banks.get()
                    tC = pe_job([(lambda h, k=k, bc=bc, cl=cl, uc=uc, Tm=Tm: h.matmul(
                        PS[:, bc, 0:Tm], uc.v[:, k, cl * 128:(cl + 1) * 128], hTm[:, k, 0:Tm],
                        start=(k == 0), stop=(k == KC - 1))) for k in range(KC)], [uc.ld, t_hm, rdc])
                    bx, rdx = banks.get()
                    tX = pe_job([(lambda h, k=k, bx=bx, cl=cl, ux=ux, Tm=Tm: h.matmul(
                        PS[:, bx, 0:Tm], ux.v[:, k, cl * 128:(cl + 1) * 128], hTm[:, k, 0:Tm],
                        start=(k == 0), stop=(k == KC - 1))) for k in range(KC)], [ux.ld, rdx])
                    bb_, rdb = banks.get()
                    tB = pe_job([(lambda h, k=k, bb_=bb_, cl=cl, ub=ub, Tm=Tm: h.matmul(
                        PS[:, bb_, 0:Tm], ub.v[:, k, cl * 128:(cl + 1) * 128], hTm[:, k, 0:Tm],
                        start=(k == 0), stop=(k == KC - 1))) for k in range(KC)], [ub.ld, rdb])
                    t_c = P.op("act", (lambda h, bc=bc, Tm=Tm: h.activation(out=TC[:, 0:Tm], in_=PS[:, bc, 0:Tm],
                                                                      func=AF.Identity)),
                               waits=[tC, tc_free], sem=s_act)
                    banks.set_reader(bc, t_c)
                    t_y = None
                    for sg in segs:
                        c0, n, zo, sq_ = sg["c0"], sg["n"], sg["zo"], sg["seq"]
                        P.op("dve", (lambda h, zo=zo, sq_=sq_, c=c: h.tensor_copy(out=Z[:, zo - 2:zo],
                                                                                   in_=CARRY[:, sq_, c, :])),
                             waits=[z_free, carry_t[sq_]])
                        t_z = P.op("dve", (lambda h, zo=zo, n=n, c0=c0, bx=bx: h.tensor_tensor(
                            out=Z[:, zo:zo + n], in0=TC[:, c0:c0 + n], in1=PS[:, bx, c0:c0 + n], op=ALU.mult)),
                                   waits=[t_c, tX], sem=s_dve)
                        t_cr = P.op("dve", (lambda h, zo=zo, nr=sg["nr"], sq_=sq_, c=c: h.tensor_copy(
                            out=CARRY[:, sq_, c, :], in_=Z[:, zo + nr - 2:zo + nr])), hard=[t_z], sem=s_dve)
                        sg["tcr"] = t_cr
                        P.op("dve", (lambda h, zo=zo, n=n, c0=c0, c=c: h.tensor_scalar_mul(
                            out=Y[:, c0:c0 + n], in0=Z[:, zo - 2:zo - 2 + n], scalar1=CW[:, c, 0:1])),
                             waits=[y_free])
                        P.op("dve", (lambda h, zo=zo, n=n, c0=c0, c=c: h.scalar_tensor_tensor(
                            out=Y[:, c0:c0 + n], in0=Z[:, zo - 1:zo - 1 + n], scalar=CW[:, c, 1:2],
                            in1=Y[:, c0:c0 + n], op0=ALU.mult, op1=ALU.add)))
                        P.op("dve", (lambda h, zo=zo, n=n, c0=c0, c=c: h.scalar_tensor_tensor(
                            out=Y[:, c0:c0 + n], in0=Z[:, zo:zo + n], scalar=CW[:, c, 2:3],
                            in1=Y[:, c0:c0 + n], op0=ALU.mult, op1=ALU.add)))
                        t_y = P.op("dve", (lambda h, n=n, c0=c0, bb_=bb_: h.tensor_tensor(
                            out=Y[:, c0:c0 + n], in0=Y[:, c0:c0 + n], in1=PS[:, bb_, c0:c0 + n], op=ALU.mult)),
                                   waits=[tB], sem=s_dve)
                    banks.set_reader(bx, t_y)
                    banks.set_reader(bb_, t_y)
                    tc_free = t_y
                    z_free = t_y
                    t_sq = P.op("act", (lambda h, Tm=Tm: h.activation(out=SQ[:, 0:Tm], in_=Y[:, 0:Tm], func=AF.Square)),
                                waits=[t_y, sq_free], sem=s_act)
                    t_ya = P.op("act", (lambda h, c=c, Tm=Tm: h.activation(out=yaT[:, c, 0:Tm], in_=Y[:, 0:Tm],
                                                                     func=AF.Identity, scale=GA[:, c:c + 1])),
                                waits=dst_free, sem=s_act)
                    y_free = t_ya
                    for li, (i, ntok, lc) in enumerate(mcols):
                        col = ((0 * MT + li) * NCC + c) * 2
                        sq_free = P.op("pe", (lambda h, ntok=ntok, lc=lc, col=col: h.matmul(
                            PS[:ntok, STATB, col:col + 2], SQ[:, lc:lc + ntok], ONES2[:, :], start=True, stop=True)),
                                       waits=[t_sq, stat_free[0]], sem=s_pe)
                    t_sa = sq_free
                ring.release(uc, tC)
                ring.release(ux, tX)
                ring.release(ub, tB)
            for sg in segs:
                sq_ = sg["seq"]
                carry_t[sq_] = sg["tcr"]
                lst = sg["last"]
                if lst.sample or lst.t == cfg.NT - 1:
                    dstd = css if lst.sample else csp[sq_]
                    for c in range(NCC):
                        t = P.op("pool", (lambda h, c=c, sq_=sq_, dstd=dstd: h.dma_start(
                            out=dstd[:, c * 128:(c + 1) * 128].rearrange("r p -> p r"), in_=CARRY[:, sq_, c, :])),
                                 waits=[sg["tcr"]], sem=s_mst, amt=16)
            t_sb = None
            for pr in range(DG // UW):
                uu_ = ring.acquire((mtag, "uu", pr))
                tUj = None
                for cl in range(2):
                    g = 2 * pr + cl
                    bu, rdu = banks.get()
                    tUj = pe_job([(lambda h, k=k, bu=bu, cl=cl, uu_=uu_, Tm=Tm: h.matmul(
                        PS[:, bu, 0:Tm], uu_.v[:, k, cl * 128:(cl + 1) * 128], hTm[:, k, 0:Tm],
                        start=(k == 0), stop=(k == KC - 1))) for k in range(KC)], [uu_.ld, t_hm, rdu])
                    bm, rdm = banks.get()
                    tMj = pe_job([(lambda h, bm=bm, g=g, li=li, ntok=ntok, lc=lc: h.matmul(
                        PS[:, bm, lc:lc + ntok], Vn[:ntok, li, g * 128:(g + 1) * 128], WMT[:ntok, g, 0:ntok],
                        start=True, stop=True)) for li, (i, ntok, lc) in enumerate(mcols)], [t_vall, rdm, t_w])
                    t_gu = P.op("act", (lambda h, bu=bu, Tm=Tm: h.activation(out=TC[:, 0:Tm], in_=PS[:, bu, 0:Tm],
                                                                       func=AF.Gelu_apprx_tanh)),
                                waits=[tUj, tc_free], sem=s_act)
                    banks.set_reader(bu, t_gu)
                    t_y = None
                    for li, (i, ntok, lc) in enumerate(mcols):
                        P.op("dve", (lambda h, bm=bm, g=g, ntok=ntok, lc=lc: h.tensor_tensor(
                            out=Y[:, lc:lc + ntok], in0=PS[:, bm, lc:lc + ntok], in1=BB[:, g, 0:ntok], op=ALU.add)),
                             waits=[tMj, y_free])
                    t_y = P.op("dve", (lambda h, Tm=Tm: h.tensor_tensor(out=Y[:, 0:Tm], in0=Y[:, 0:Tm], in1=TC[:, 0:Tm],
                                                                  op=ALU.mult)), waits=[t_gu], sem=s_dve)
                    banks.set_reader(bm, t_y)
                    tc_free = t_y
                    t_sq = P.op("act", (lambda h, Tm=Tm: h.activation(out=SQ[:, 0:Tm], in_=Y[:, 0:Tm], func=AF.Square)),
                                waits=[t_y, sq_free], sem=s_act)
                    t_yb = P.op("act", (lambda h, g=g, Tm=Tm: h.activation(out=ybT[:, g, 0:Tm], in_=Y[:, 0:Tm],
                                                                     func=AF.Identity, scale=GB[:, g:g + 1])),
                                waits=dst_free, sem=s_act)
                    y_free = t_yb
                    for li, (i, ntok, lc) in enumerate(mcols):
                        col = ((1 * MT + li) * NCC + g) * 2
                        sq_free = P.op("pe", (lambda h, ntok=ntok, lc=lc, col=col: h.matmul(
                            PS[:ntok, STATB, col:col + 2], SQ[:, lc:lc + ntok], ONES2[:, :], start=True, stop=True)),
                                       waits=[t_sq], sem=s_pe)
                    t_sb = sq_free
                ring.release(uu_, tUj)
            t_r = None
            t_st = None
            for li, (i, ntok, lc) in enumerate(mcols):
                for ab in range(2):
                    c0 = ((ab * MT + li) * NCC) * 2
                    t_st = P.op("dve", (lambda h, ntok=ntok, c0=c0, ab=ab, li=li: h.tensor_reduce(
                        out=ST2[:ntok, ab, li:li + 1],
                        in_=PS[:ntok, STATB, c0:c0 + 2 * NCC].rearrange("p (c t) -> p c t", t=2)[:, :, 0],
                        axis=AX.X, op=ALU.add)), waits=[t_sa, t_sb], sem=s_dve)
            stat_free[0] = t_st
            for li, (i, ntok, lc) in enumerate(mcols):
                t1 = P.op("act", (lambda h, ntok=ntok, li=li: h.activation(
                    out=SD[:ntok, 0:2], in_=ST2[:ntok, :, li], func=AF.Sqrt, scale=1.0 / DC, bias=EPSB[:ntok, :])),
                          waits=[t_st, sd_free[0]], sem=s_act)
                t_r = P.op("dve", (lambda h, ntok=ntok, li=li: h.reciprocal(out=RAB[:ntok, :, li], in_=SD[:ntok, 0:2])),
                           waits=[t1], sem=s_dve)
                sd_free[0] = t_r
            t_xo = None
            last_pe = None
            for oi in range(D // UW):
                uo = ring.acquire((mtag, "o", oi))
                tj = None
                for li, (i, ntok, lc) in enumerate(mcols):
                    for ab, src, nk in ((0, yaT, NCC), (1, ybT, NG)):
                        b, rd = banks.get()
                        koff = 0 if ab == 0 else NCC
                        tj = pe_job([(lambda h, k=k, b=b, ntok=ntok, lc=lc, src=src, koff=koff, uo=uo, nk=nk: h.matmul(
                            PS[:ntok, b, 0:UW], src[:, k, lc:lc + ntok], uo.v[:, koff + k, :],
                            start=(k == 0), stop=(k == nk - 1))) for k in range(nk)], [uo.ld, t_yb, t_ya, rd])
                        t_xo = P.op("dve", (lambda h, b=b, i=i, ntok=ntok, oi=oi, ab=ab, li=li: h.scalar_tensor_tensor(
                            out=X[:ntok, i, oi * UW:(oi + 1) * UW], in0=PS[:ntok, b, 0:UW],
                            scalar=RAB[:ntok, ab, li:li + 1], in1=X[:ntok, i, oi * UW:(oi + 1) * UW],
                            op0=ALU.mult, op1=ALU.add)), waits=[tj, x_nr], hard=[t_r], sem=s_dve)
                        banks.set_reader(b, t_xo)
                ring.release(uo, tj)
                last_pe = tj
            t_x = t_xo
            reg_free = [last_pe]
            if pi == 0 and si == 0:
                getattr(cfg, "dbg", False) and dump("x2", X[:, 0:nsl, :], [t_x])
                getattr(cfg, "dbg", False) and dump("yaT", yaT.bitcast(F32)[:, :, 0:Tm], [t_x])
                getattr(cfg, "dbg", False) and dump("ybT", ybT.bitcast(F32)[:, :, 0:Tm], [t_x])
                getattr(cfg, "dbg", False) and dump("rab", RAB[:, :, :], [t_x])
                getattr(cfg, "dbg", False) and dump("gvbc", GVBC[:, :], [t_x])
                getattr(cfg, "dbg", False) and dump("bb", BB[:, :, :], [t_x])
                getattr(cfg, "dbg", False) and dump("carry", CARRY[:].rearrange("p s c r -> p (s c r)"), [t_x])
        dst_free[:] = list(reg_free)
        xr = {i: [t_x] for i in range(nsl)}
        t_ht = norm_T(cols, G3, hT, xr)
        x_nr = x_last_read[cols[-1][0]]
        t_x, at_free, t_gul = ffn((pi, 2), tiles, cols, Tp, t_ht, x_nr)
        reg_free = [t for t in at_free if t is not None]
        if bf:
            GBCF = REG.bitcast(F32)[:, 0:D]
            GBC = GBCF
            JUNK = REG[:, 2 * D:3 * D]
        else:
            GBC = REG[:, 0:D]
            GBCF = GBC.bitcast(F32)
            JUNK = REG[:, D:2 * D]
        t_gb = P.op("pool", (lambda h: h.dma_start(out=GBC, in_=gfd.partition_broadcast(128))),
                    waits=[t_gul], sem=s_gb, amt=16)
        for (i, ntok, co) in cols:
            st = tiles[i]
            tq = P.op("act", (lambda h, i=i, ntok=ntok: h.activation(out=JUNK[:ntok, :], in_=X[:ntok, i, :],
                                                                      func=AF.Square, accum_out=SS[:ntok, i:i + 1])),
                      waits=[t_x, t_gul], sem=s_act)
            t1 = P.op("act", (lambda h, i=i, ntok=ntok: h.activation(out=SD[:ntok, i:i + 1], in_=SS[:ntok, i:i + 1],
                                                                      func=AF.Sqrt, scale=1.0 / D, bias=EPSB[:ntok, :])),
                      waits=[sd_free[0]], hard=[tq], sem=s_act)
            sd_free[0] = None
            t2 = P.op("dve", (lambda h, i=i, ntok=ntok: h.reciprocal(out=RS[:ntok, i:i + 1], in_=SD[:ntok, i:i + 1])),
                      waits=[t1], sem=s_dve)
            tf = P.op("dve", (lambda h, i=i, ntok=ntok: h.scalar_tensor_tensor(
                out=XN[:ntok, :], in0=X[:ntok, i, :], scalar=RS[:ntok, i:i + 1], in1=GBCF[:ntok, :],
                op0=ALU.mult, op1=ALU.mult)), waits=[t_gb] + xn_free, hard=[t2], sem=s_dve)
            x_free[i] = tf
            dstd = ys[:, :] if st.sample else yp[st.seq, st.t * 128:(st.t + 1) * 128, :]
            ts = P.op("pool", (lambda h, st=st, dstd=dstd: h.dma_start(out=dstd, in_=XN[:st.nreal, :])),
                      waits=[tf], sem=s_yst, amt=16)
            xn_free[:] = [ts]
    P.op("pool", lambda h: None, waits=[(s_yst, s_yst.n), (s_mst, s_mst.n), (s_dbg, s_dbg.n)])
    assert ring.next_use == len(useq)

    with nc.Block() as block:
        @block.sync
        def _(h):
            emit(h, P.q["sync"])

        @block.scalar
        def _(h):
            emit(h, P.q["act"])

        @block.vector
        def _(h):
            emit(h, P.q["dve"])

        @block.gpsimd
        def _(h):
            with nc.allow_non_contiguous_dma(reason="tiny constant / state layouts"):
                emit(h, P.q["pool"])

        @block.tensor
        def _(h):
            emit(h, P.q["pe"])
    for cm in sems.values():
        cm.__exit__(None, None, None)
    return nc


stat_free = [None]
sd_free = [None]


def make_in_maps(cfg, inputs, ncores):
    f = lambda a: np.ascontiguousarray(np.asarray(a, dtype=np.float32))
    shared = {
        "wg1": f(inputs["ffn1_w_gate"][0]), "wu1": f(inputs["ffn1_w_up"][0]), "wd1": f(inputs["ffn1_w_down"][0]),
        "win": f(inputs["w_in"][0]), "wo": f(inputs["w_o"][0]),
        "wg2": f(inputs["ffn2_w_gate"][0]), "wu2": f(inputs["ffn2_w_up"][0]), "wd2": f(inputs["ffn2_w_down"][0]),
        "g1": f(inputs["ffn1_norm"][0]), "g2": f(inputs["mix_norm"][0]), "g3": f(inputs["ffn2_norm"][0]),
        "gf": f(inputs["final_norm"]).reshape(-1),
        "cw": f(inputs["conv_w"][0]), "gv": f(inputs["gmlp_v_norm"][0]), "ws": f(inputs["gmlp_w_s"][0]),
        "bs": f(inputs["gmlp_b"][0]), "ga": f(inputs["conv_out_norm"][0]), "gb": f(inputs["gmlp_out_norm"][0]),
    }
    xp, xs, cc = f(inputs["x_prompt"]), f(inputs["x_sample"]), f(inputs["cache_conv"][0])
    maps = []
    for c in range(ncores):
        m = dict(shared)
        m["xp"] = xp[c * cfg.NSEQ:(c + 1) * cfg.NSEQ]
        m["xs"] = xs[c]
        m["cc"] = cc[c]
        maps.append(m)
    return maps


_CACHE = {}


def kernel(**inputs):
    cfg = Cfg(fast=FAST, PT=9, MT=4, R=5) if BF16_MM else Cfg(fast=FAST)
    ncores = 8
    if "nc" not in _CACHE:
        stat_free[0] = None
        sd_free[0] = None
        _CACHE["nc"] = build_program(cfg)
    nc = _CACHE["nc"]
    maps = make_in_maps(cfg, inputs, ncores)
    res = run_bass_kernel_spmd(nc, maps, core_ids=list(range(ncores)))
    r = res.results
    y_prompt = np.concatenate([r[c]["yp"] for c in range(ncores)], axis=0)
    y_sample = np.stack([r[c]["ys"] for c in range(ncores)], axis=0)
    csp = np.concatenate([r[c]["csp"] for c in range(ncores)], axis=0)[None]
    css = np.stack([r[c]["css"] for c in range(ncores)], axis=0)[None]
    gvs = np.stack([r[c]["gvs"] for c in range(ncores)], axis=0)[None]
    return (y_prompt.astype(np.float32), y_sample.astype(np.float32), csp.astype(np.float32),
            css.astype(np.float32), gvs.astype(np.float32))
```
